# Optimizing a Trainium2 kernel written in Bass

```python
import math
import jax, jax.numpy as jnp
from jax import lax
import numpy as np

D_MODEL = 2048
BATCH = 4
SEQ = 2048
DEPTH = 4

N_MIXERS = 3
HEAD_DIM = 64
N_HEADS = D_MODEL // HEAD_DIM
BRANCH = N_HEADS * HEAD_DIM
N_KV_A = N_HEADS // 8
KV_A = N_KV_A * HEAD_DIM
WINDOW = 128
BLOCK = 128
NORM_EPS = 1e-6
NEG = -1e30

A_COLS = BRANCH + 2 * KV_A + BRANCH
B_COLS = 4 * BRANCH
C_COLS = 4 * BRANCH + N_HEADS

kernel_name = "hybrid_swa_stickbreak_fox_trunk"


def _n_layers_of(kind):
    return sum(1 for i in range(DEPTH) if i % N_MIXERS == kind)


def _alibi_slopes(n):
    return jnp.asarray(2.0 ** (-8.0 * np.arange(1, n + 1, dtype=np.float32) / n), dtype=jnp.float32)


def rmsnorm(x, g):
    xf = x.astype(jnp.float32)
    r = lax.rsqrt(jnp.mean(xf * xf, axis=-1, keepdims=True) + NORM_EPS)
    return (xf * r * g.astype(jnp.float32)).astype(x.dtype)


def swa_sink_mixer(h, w_in, sinks):
    B, S, _ = h.shape
    nb = S // BLOCK
    G = N_HEADS // N_KV_A
    q, k, v, z = jnp.split(h @ w_in, [BRANCH, BRANCH + KV_A, BRANCH + 2 * KV_A], axis=-1)
    q = q.reshape(B, nb, BLOCK, N_KV_A, G, HEAD_DIM).astype(jnp.float32)
    k = k.reshape(B, nb, BLOCK, N_KV_A, HEAD_DIM).astype(jnp.float32)
    v = v.reshape(B, nb, BLOCK, N_KV_A, HEAD_DIM).astype(jnp.float32)

    def band(a):
        prev = jnp.concatenate([jnp.zeros_like(a[:, :1]), a[:, :-1]], axis=1)
        return jnp.concatenate([prev, a], axis=2)

    kb, vb = band(k), band(v)
    scores = jnp.einsum('bnqhgd,bnkhd->bnhgqk', q, kb) * (HEAD_DIM ** -0.5)
    qi = jnp.arange(BLOCK)[:, None]
    kj = jnp.arange(2 * BLOCK)[None, :]
    dist = (qi + BLOCK - kj).astype(jnp.float32)
    s_pos = jnp.arange(nb)[:, None, None] * BLOCK - BLOCK + kj[None]
    valid = (dist >= 0) & (dist < WINDOW) & (s_pos >= 0)
    slopes = _alibi_slopes(N_HEADS).reshape(N_KV_A, G)
    scores = scores - slopes[:, :, None, None] * dist
    scores = jnp.where(valid[None, :, None, None], scores, NEG)
    sink = sinks.astype(jnp.float32).reshape(N_KV_A, G)
    sink_col = jnp.broadcast_to(sink[None, None, :, :, None, None], scores.shape[:-1] + (1,))
    p = jax.nn.softmax(jnp.concatenate([scores, sink_col], axis=-1), axis=-1)[..., :-1]
    o = jnp.einsum('bnhgqk,bnkhd->bnqhgd', p, vb).reshape(B, S, BRANCH)
    return o, z


def stick_breaking_mixer(h, w_in):
    B, S, _ = h.shape
    nb = S // BLOCK
    q, k, v, z = jnp.split(h @ w_in, 4, axis=-1)
    q = q.reshape(B, nb, BLOCK, N_HEADS, HEAD_DIM).transpose(1, 0, 3, 2, 4).astype(jnp.float32)
    k = k.reshape(B, S, N_HEADS, HEAD_DIM).astype(jnp.float32)
    v = v.reshape(B, S, N_HEADS, HEAD_DIM).astype(jnp.float32)
    s_pos = jnp.arange(S)
    scale = HEAD_DIM ** -0.5

    def block(args):
        qb, n = args
        t_pos = n * BLOCK + jnp.arange(BLOCK)
        logits = jnp.einsum('bhqd,bshd->bhqs', qb, k) * scale
        before = s_pos[None, :] < t_pos[:, None]
        log_fail = jnp.where(before, jax.nn.log_sigmoid(-logits), 0.0)
        incl = lax.cumsum(log_fail, axis=3, reverse=True)
        suffix = jnp.concatenate([incl[..., 1:], jnp.zeros_like(incl[..., :1])], axis=-1)
        a = jnp.where(before, jnp.exp(jax.nn.log_sigmoid(logits) + suffix), 0.0)
        return jnp.einsum('bhqs,bshd->bqhd', a, v)

    o = lax.map(block, (q, jnp.arange(nb)))
    o = o.transpose(1, 0, 2, 3, 4).reshape(B, S, BRANCH)
    return o, z


def forgetting_mixer(h, w_in, b_f):
    B, S, _ = h.shape
    nb = S // BLOCK
    q, k, v, z, f_logit = jnp.split(h @ w_in, [BRANCH, 2 * BRANCH, 3 * BRANCH, 4 * BRANCH], axis=-1)
    log_f = jax.nn.log_sigmoid(f_logit.astype(jnp.float32) + b_f.astype(jnp.float32))
    cum = lax.cumsum(log_f, axis=1).transpose(0, 2, 1)
    q = q.reshape(B, nb, BLOCK, N_HEADS, HEAD_DIM).transpose(1, 0, 3, 2, 4).astype(jnp.float32)
    cq = cum.reshape(B, N_HEADS, nb, BLOCK).transpose(2, 0, 1, 3)
    k = k.reshape(B, S, N_HEADS, HEAD_DIM).astype(jnp.float32)
    v = v.reshape(B, S, N_HEADS, HEAD_DIM).astype(jnp.float32)
    s_pos = jnp.arange(S)
    scale = HEAD_DIM ** -0.5

    def block(args):
        qb, cqb, n = args
        t_pos = n * BLOCK + jnp.arange(BLOCK)
        logits = jnp.einsum('bhqd,bshd->bhqs', qb, k) * scale + cqb[..., :, None] - cum[:, :, None, :]
        causal = s_pos[None, :] <= t_pos[:, None]
        p = jax.nn.softmax(jnp.where(causal, logits, NEG), axis=-1)
        return jnp.einsum('bhqs,bshd->bqhd', p, v)

    o = lax.map(block, (q, cq, jnp.arange(nb)))
    o = o.transpose(1, 0, 2, 3, 4).reshape(B, S, BRANCH)
    return o, z


def setup_inputs(seed: int = 0) -> dict:
    key = jax.random.key(seed)
    ks = jax.random.split(key, 12)
    n_a, n_b, n_c = _n_layers_of(0), _n_layers_of(1), _n_layers_of(2)
    f32 = jnp.float32
    nrm = jax.random.normal
    return {
        "x": nrm(ks[0], (BATCH, SEQ, D_MODEL), f32),
        "g_pre": 1.0 + 0.02 * nrm(ks[1], (DEPTH, D_MODEL), f32),
        "g_post": 1.0 + 0.02 * nrm(ks[2], (DEPTH, D_MODEL), f32),
        "w_in_a": nrm(ks[3], (n_a, D_MODEL, A_COLS), f32) * D_MODEL ** -0.5,
        "w_out_a": nrm(ks[4], (n_a, BRANCH, D_MODEL), f32) * BRANCH ** -0.5,
        "sinks_a": 0.5 * nrm(ks[5], (n_a, N_HEADS), f32),
        "w_in_b": nrm(ks[6], (n_b, D_MODEL, B_COLS), f32) * D_MODEL ** -0.5,
        "w_out_b": nrm(ks[7], (n_b, BRANCH, D_MODEL), f32) * BRANCH ** -0.5,
        "w_in_c": nrm(ks[8], (n_c, D_MODEL, C_COLS), f32) * D_MODEL ** -0.5,
        "b_f_c": 1.0 + 5.0 * jax.random.uniform(ks[9], (n_c, N_HEADS), f32),
        "w_out_c": nrm(ks[10], (n_c, BRANCH, D_MODEL), f32) * BRANCH ** -0.5,
    }


def reference(x, g_pre, g_post, w_in_a, w_out_a, sinks_a, w_in_b, w_out_b, w_in_c, b_f_c, w_out_c):
    for i in range(DEPTH):
        kind, j = i % N_MIXERS, i // N_MIXERS
        h = rmsnorm(x, g_pre[i])
        if kind == 0:
            o, z = swa_sink_mixer(h, w_in_a[j], sinks_a[j])
            w_out = w_out_a[j]
        elif kind == 1:
            o, z = stick_breaking_mixer(h, w_in_b[j])
            w_out = w_out_b[j]
        else:
            o, z = forgetting_mixer(h, w_in_c[j], b_f_c[j])
            w_out = w_out_c[j]
        y = (o.astype(z.dtype) * jax.nn.silu(z)) @ w_out
        x = x + rmsnorm(y, g_post[i])
    return x
```

```python
import numpy as np
import ml_dtypes
from contextlib import ExitStack

import concourse.bass as bass
import concourse.mybir as mybir
from concourse.bass_utils import run_bass_kernel_spmd

F32 = mybir.dt.float32
BF16 = mybir.dt.bfloat16
AF = mybir.ActivationFunctionType
ALU = mybir.AluOpType

D = 2048
KC = 16
T = 1024
NSLOT = 8
H = 32
HD = 64
DEPTH = 4
EPS = 1e-6
NEGBIG = -30000.0
KINDS = [0, 1, 2, 0]
KIDX = [0, 0, 0, 1]
ENGS = ("pe", "act", "dve", "pool", "sp")
STRICT_SAME_ENGINE = True


def _slopes():
    return [float(np.float32(2.0) ** np.float32(-8.0 * (h + 1) / H)) for h in range(H)]


class DSem:
    def __init__(self, h):
        self.h = h
        self.n = 0


class Op:
    __slots__ = ("eng", "fn", "deps", "sig", "ordv", "kind", "dsem", "dval", "inc")

    def __init__(self, eng, fn, kind):
        self.eng = eng
        self.fn = fn
        self.kind = kind
        self.deps = []
        self.sig = False
        self.ordv = 0
        self.dsem = None
        self.dval = 0
        self.inc = 0


class Prog:
    def __init__(self, nc, es):
        self.nc = nc
        self.es = es
        self.ops = {e: [] for e in ENGS}
        self.last_w = {}
        self.readers = {}
        self.psem = {e: es.enter_context(nc.semaphore("prog_" + e)) for e in ENGS}
        self.nsem = 0

    def dsem(self, name):
        self.nsem += 1
        return DSem(self.es.enter_context(self.nc.semaphore(name)))

    def op(self, eng, fn, reads=(), writes=(), dsem=None, inc=16, extra=()):
        kind = "d" if dsem is not None else "c"
        o = Op(eng, fn, kind)
        deps = {}
        raw = set()
        for k in reads:
            w = self.last_w.get(k)
            if w is not None:
                deps[id(w)] = w
                raw.add(id(w))
        for k in writes:
            w = self.last_w.get(k)
            if w is not None:
                deps[id(w)] = w
            rd = self.readers.get(k)
            if rd:
                for r in rd.values():
                    if isinstance(r, list):
                        for rr in r:
                            deps[id(rr)] = rr
                    else:
                        deps[id(r)] = r
        for d in extra:
            deps[id(d)] = d
            raw.add(id(d))
        for i, d in deps.items():
            if d is o:
                continue
            if kind == "c" and d.kind == "c" and d.eng == eng:
                if eng == "pe" or (STRICT_SAME_ENGINE is False and i not in raw):
                    continue
            o.deps.append(d)
            if d.kind == "c":
                d.sig = True
        if kind == "d":
            dsem.n += inc
            o.dsem = dsem
            o.dval = dsem.n
            o.inc = inc
        for k in writes:
            self.last_w[k] = o
            self.readers[k] = {}
        for k in reads:
            rd = self.readers.setdefault(k, {})
            if kind == "d":
                rd.setdefault("dma", []).append(o)
            else:
                rd[eng] = o
        self.ops[eng].append(o)
        return o

    def emit(self, final_waits):
        nc = self.nc
        for e in ENGS:
            c = 0
            for o in self.ops[e]:
                if o.kind == "c" and o.sig:
                    c += 1
                    o.ordv = c
        psem = self.psem

        def run(eng_name, eng):
            waited = {}
            for o in self.ops[eng_name]:
                needs = {}
                for d in o.deps:
                    if d.kind == "c":
                        s, v = psem[d.eng], d.ordv
                    else:
                        s, v = d.dsem.h, d.dval
                    key = id(s)
                    if key not in needs or needs[key][1] < v:
                        needs[key] = (s, v)
                for key, (s, v) in needs.items():
                    if waited.get(key, 0) < v:
                        eng.wait_ge(s, v)
                        waited[key] = v
                ins = o.fn(eng)
                if o.kind == "c":
                    if o.sig:
                        ins.then_inc(psem[eng_name], 1)
                else:
                    ins.then_inc(o.dsem.h, o.inc)
            for ds in final_waits.get(eng_name, ()):
                eng.wait_ge(ds.h, ds.n)

        with nc.Block() as block:
            @block.tensor
            def _(e):
                run("pe", e)

            @block.scalar
            def _(e):
                run("act", e)

            @block.vector
            def _(e):
                run("dve", e)

            @block.gpsimd
            def _(e):
                run("pool", e)

            @block.sync
            def _(e):
                run("sp", e)


def layer_tile_plan(l):
    kind = KINDS[l]
    nkv = 4 if kind == 0 else 16
    plan = [("k", i) for i in range(nkv)] + [("v", i) for i in range(nkv)]
    if kind == 2:
        plan.append(("f", 0))
    for hp in range(16):
        plan.append(("q", hp))
        plan.append(("z", hp))
    plan += [("o", m) for m in range(16)]
    return plan


def tile_cols(l, kind, idx):
    k = KINDS[l]
    if kind == "o":
        return np.arange(idx * 128, (idx + 1) * 128)
    if k == 0:
        offs = {"q": 0, "k": 2048, "v": 2304, "z": 2560}
        if kind in ("k", "v"):
            c = offs[kind] + idx * 64 + np.arange(64)
            return np.concatenate([c, c])
    else:
        offs = {"q": 0, "k": 2048, "v": 4096, "z": 6144, "f": 8192}
    if kind == "f":
        return offs["f"] + np.arange(32)
    return offs[kind] + idx * 128 + np.arange(128)


def build_program(layers):
    nc = bass.Bass("TRN2", target_bir_lowering=False)
    es = ExitStack()
    P = Prog(nc, es)

    def dram_in(name, shape, dt=F32):
        return nc.dram_tensor(name, list(shape), dt, kind="ExternalInput").ap()

    xT_d = dram_in("xT", [128, KC * T])
    out_d = nc.dram_tensor("outT", [128, KC * T], F32, kind="ExternalOutput").ap()
    gpre_d = dram_in("gpre", [128, DEPTH * KC])
    gpost_d = dram_in("gpost", [128, DEPTH * KC])
    w_d = {}
    plans = {}
    for l in layers:
        plans[l] = layer_tile_plan(l)
        w_d[l] = dram_in(f"w{l}", [len(plans[l]), 2, 128, KC * 64])
    cbf_d = dram_in("cbf", [128, 12 * 128], BF16)
    maskSB_d = dram_in("maskSB", [128, 2 * 128], BF16)
    maskFX_d = dram_in("maskFX", [128, 2 * 128], BF16)
    distA_d = dram_in("distA", [128, 3 * 128])
    sinks_d = dram_in("sinks", [128, 2 * 16])
    bfb_d = dram_in("bfb", [128, 8 * 32])
    par_d = dram_in("par", [128, 2])

    send_k, recv_k, send_v, recv_v = {}, {}, {}, {}
    for l in layers:
        n = 4 if KINDS[l] == 0 else 16
        send_k[l] = nc.dram_tensor(f"send_k{l}", [n * 128, T], BF16)
        recv_k[l] = nc.dram_tensor(f"recv_k{l}", [2 * n * 128, T], BF16)
        send_v[l] = nc.dram_tensor(f"send_v{l}", [n * 128, T], BF16)
        recv_v[l] = nc.dram_tensor(f"recv_v{l}", [2 * n * 128, T], BF16)
    send_f = nc.dram_tensor("send_f", [128, 256], F32)
    recv_f = nc.dram_tensor("recv_f", [256, 256], F32)

    sbtot = {"b": 0}

    def sb(name, shape, dt):
        sbtot["b"] += int(np.prod(shape[1:])) * (4 if dt == F32 else 2)
        return es.enter_context(nc.sbuf_tensor("s_" + name, list(shape), dt))

    def ps(name, shape, dt=F32):
        return es.enter_context(nc.psum_tensor("p_" + name, list(shape), dt))

    xT = sb("xT", [128, KC, T], F32)
    hT = sb("hT", [128, KC, T], BF16)
    ytmp = hT.bitcast(F32).reshape([128, KC, 512])
    GT = sb("GT", [128, KC, T], BF16)
    stage = [sb(f"stage{i}", [128, KC, 64], F32) for i in range(2)]
    wb = [sb(f"wb{i}", [128, KC, 128], BF16) for i in range(2)]
    NBP = 2
    Kpair = [sb(f"Kpair{i}", [128, 2 * T], BF16) for i in range(NBP)]
    Vpair = [sb(f"Vpair{i}", [128, 2, 8, 128], BF16) for i in range(NBP)]
    QA = [sb(f"QA{i}", [128, T], BF16) for i in range(NBP)]
    QB = [sb(f"QB{i}", [128, T], BF16) for i in range(NBP)]
    SZ = [sb(f"SZ{i}", [128, T], BF16) for i in range(NBP)]
    maskb = sb("maskb", [128, 2, 128], BF16)
    cbf = sb("cbf", [128, 12, 128], BF16)
    gpre = sb("gpre", [128, DEPTH * KC], F32)
    gpost = sb("gpost", [128, DEPTH * KC], F32)
    distA = sb("distA", [128, 3 * 128], F32)
    sinks = sb("sinks", [128, 32], F32)
    esink = sb("esink", [128, 32], F32)
    bfb = sb("bfb", [128, 256], F32)
    par = sb("par", [128, 2], F32)
    NST = 1
    KTst = [sb(f"KTst{i}", [128, T], BF16) for i in range(NST)]
    Vst = [sb(f"Vst{i}", [128, 8, 128], BF16) for i in range(NST)]
    f32a = [sb(f"f32a{i}", [128, 512], F32) for i in range(2)]
    bfa = [sb(f"bfa{i}", [128, 512], BF16) for i in range(2)]
    bfp = [sb(f"bfp{i}", [128, 512], BF16) for i in range(2)]
    Sacc = [sb(f"Sacc{i}", [128, 512], F32) for i in range(2)]
    Saccb = [sb(f"Saccb{i}", [128, 512], BF16) for i in range(2)]
    cqb = Sacc
    rl = sb("rl", [128, 512], F32)
    rstd = rl
    zs = sb("zs", [128, 512], F32)
    zsv = zs.reshape([128, 16, 32])
    cq_own = zsv[:, 0:8, :]
    cq_tmp = zsv[:, 8:16, :]
    sqb = bfa
    lf_own = sb("lf_own", [128, 8, 32], F32)
    lf_full = Sacc[0].reshape([128, 16, 32])
    totb = Sacc[1].reshape([128, 16, 32])
    carry = rl.reshape([128, 16, 32])
    cum = sb("cum", [128, 16, 32], F32)
    ncum = sb("ncum", [128, 16, 32], F32)

    print("SBUF bytes/partition:", sbtot["b"])
    pj = [ps(f"pj{i}", [128, 512]) for i in range(2)]
    pz = [ps(f"pz{i}", [128, 512]) for i in range(3)]
    pso = [ps(f"pso{i}", [128, 512]) for i in range(2)]
    pl = ps("pl", [128, 512])
    pl_bf = pl.bitcast(BF16).reshape([128, 8, 128])

    ident_bf, ones_bf, negtri_bf, negones_bf, oneslo_bf, oneshi_bf = [cbf[:, i, :] for i in range(6)]
    triup_bf = cbf[:, 6, :]
    ident4_bf = cbf[:, 7:11, :]
    zeros_bf = cbf[:, 11, :]
    ident4_flat = cbf[:, 7:11, :].rearrange("p a t -> p (a t)")
    cqs = sb("cqs", [128, 3, 256], BF16)

    s_init = P.dsem("s_init")
    s_st = [P.dsem(f"s_st{i}") for i in range(2)]
    s_kst = [P.dsem(f"s_kst{i}") for i in range(2)]
    s_vst = [P.dsem(f"s_vst{i}") for i in range(2)]
    s_kv = [P.dsem(f"s_kv{i}") for i in range(2)]
    s_cck = P.dsem("s_cck")
    s_ccv = P.dsem("s_ccv")
    s_ccf = P.dsem("s_ccf")
    s_f = P.dsem("s_f")
    s_f2 = P.dsem("s_f2")
    s_mask = P.dsem("s_mask")
    s_out = P.dsem("s_out")

    RG = [[0, 1], [2, 3], [4, 5], [6, 7]]

    init_keys = ["cbf", "cf32", "gpre", "gpost", "distA", "sinks", "bfb", "par"]

    def dma(eng, out, in_, dsem, reads=(), writes=(), extra=()):
        return P.op(eng, lambda e, o=out, i=in_: e.dma_start(out=o, in_=i), reads=reads, writes=writes,
                    dsem=dsem, extra=extra)

    for q in range(4):
        dma("sp", xT[:, 4 * q:4 * q + 4, :], xT_d[:, 4 * q * T:(4 * q + 4) * T].rearrange("p (a t) -> p a t", a=4),
            s_init, writes=[("xT", kc, hf) for kc in range(4 * q, 4 * q + 4) for hf in range(2)])
    for name, sbt, dr in (("cbf", cbf, cbf_d),):
        dma("sp", sbt[:, :, :], dr.rearrange("p (a t) -> p a t", t=128), s_init, writes=[name])
    for name, sbt, dr in (("gpre", gpre, gpre_d), ("gpost", gpost, gpost_d), ("distA", distA, distA_d),
                          ("sinks", sinks, sinks_d), ("bfb", bfb, bfb_d), ("par", par, par_d)):
        dma("sp", sbt[:, :], dr, s_init, writes=[name])
    init_tok = P.ops["sp"][-1]

    first_done = set()

    def C(eng, fn, reads=(), writes=(), extra=()):
        ex = list(extra)
        if eng not in first_done:
            first_done.add(eng)
            ex.append(init_tok)
        return P.op(eng, fn, reads=reads, writes=writes, extra=ex)

    for i in range(NBP):
        C("pool", lambda e, t=QA[i]: e.memset(t[:, :], 0.0), writes=[("QA", i, 0), ("QA", i, 1)])
        C("pool", lambda e, t=QB[i]: e.memset(t[:, :], 0.0), writes=[("QB", i, 0), ("QB", i, 1)])
    C("act", lambda e: e.activation(out=esink[:, :], in_=sinks[:, :], func=AF.Exp), writes=["esink"])

    wlist = []
    for l in layers:
        plan = plans[l]
        pos = {kv: i for i, kv in enumerate(plan)}
        kind = KINDS[l]
        nkv = 4 if kind == 0 else 16
        order = [("k", i) for i in range(nkv)] + [("v", i) for i in range(nkv)]
        if kind == 2:
            order.append(("f", 0))
        for hp in range(16):
            order += [("q", hp), ("z", hp)]
        order += [("o", m) for m in range(16)] * 2
        for kv in order:
            wlist.append((l, pos[kv]))
    wstate = {"fetched": 0}

    def wfetch(i):
        l, t = wlist[i]
        s = i % 2
        tk = plans[l][t][0]
        for hf in range(2):
            dma("sp", stage[hf][:, :, :], w_d[l][t, hf].rearrange("p (a n) -> p a n", a=KC), s_st[hf],
                writes=[("stage", hf)])
            if tk in ("q", "z"):
                C("pool", lambda e, s=s, hf=hf: e.tensor_copy(out=wb[s][:, :, hf * 64:(hf + 1) * 64],
                                                              in_=stage[hf][:, :, :]),
                  reads=[("stage", hf)], writes=[("wb", s, hf)])
            elif hf == 0:
                C("dve", lambda e, s=s, hf=hf: e.tensor_copy(out=wb[s][:, :, hf * 64:(hf + 1) * 64],
                                                             in_=stage[hf][:, :, :]),
                  reads=[("stage", hf)], writes=[("wb", s, hf)])
            else:
                C("act", lambda e, s=s, hf=hf: e.activation(out=wb[s][:, :, hf * 64:(hf + 1) * 64],
                                                            in_=stage[hf][:, :, :], func=AF.Copy),
                  reads=[("stage", hf)], writes=[("wb", s, hf)])

    wcur = {"i": 0}

    def wneed():
        i = wcur["i"]
        wcur["i"] += 1
        while wstate["fetched"] < min(len(wlist), i + 2):
            wfetch(wstate["fetched"])
            wstate["fetched"] += 1
        return i % 2

    pjc = {"i": 0}

    def proj(wslot, rhs_of_kc, rhs_keys_of_kc, ncol=128, half=None):
        b = pjc["i"] % 2
        pjc["i"] += 1
        for kc in range(KC):
            rhs = rhs_of_kc(kc)
            C("pe", lambda e, b=b, kc=kc, w=wslot, rhs=rhs: e.matmul(pj[b][0:ncol, :], lhsT=wb[w][:, kc, 0:ncol],
                                                                 rhs=rhs, start=(kc == 0), stop=(kc == KC - 1)),
              reads=[("wb", wslot, 0), ("wb", wslot, 1)] + rhs_keys_of_kc(kc), writes=[("pj", b)])
        return b

    def proj_gen(wslot, rhs_of_kc, rhs_keys_of_kc, ncol=128, chunk=4):
        b = pjc["i"] % 2
        pjc["i"] += 1
        for kc in range(KC):
            rhs = rhs_of_kc(kc)
            C("pe", lambda e, b=b, kc=kc, w=wslot, rhs=rhs: e.matmul(pj[b][0:ncol, :], lhsT=wb[w][:, kc, 0:ncol],
                                                                 rhs=rhs, start=(kc == 0), stop=(kc == KC - 1)),
              reads=[("wb", wslot, 0), ("wb", wslot, 1)] + rhs_keys_of_kc(kc), writes=[("pj", b)])
            if kc % chunk == chunk - 1 and kc != KC - 1:
                yield None
        return b

    def hrhs(half):
        return (lambda kc: hT[:, kc, half * 512:(half + 1) * 512]), (lambda kc: [("hT", kc, half)])

    def rmsnorm_stats(src_of, src_keys, n, accum_bank_key, accum_ap):
        pass

    slopes = _slopes()
    na_seen = 0
    for l in layers:
        kind = KINDS[l]
        nkv = 4 if kind == 0 else 16
        gcol = l * KC

        if kind in (1, 2):
            src = maskSB_d if kind == 1 else maskFX_d
            dma("sp", maskb[:, :, :], src.rearrange("p (a t) -> p a t", a=2), s_mask, writes=["maskb"])

        for half in range(2):
            cs = slice(half * 512, (half + 1) * 512)
            for kc in range(KC):
                sq = sqb[kc % 2]
                C("act", lambda e, kc=kc, cs=cs, sq=sq: e.activation(out=sq[:, :], in_=xT[:, kc, cs], func=AF.Square),
                  reads=[("xT", kc, half)], writes=[("bfa", kc % 2)])
                C("pe", lambda e, kc=kc, sq=sq: e.matmul(pl[:, :], lhsT=ones_bf, rhs=sq[:, :], start=(kc == 0),
                                                       stop=(kc == KC - 1)),
                  reads=[("bfa", kc % 2)], writes=["pl"])
            C("act", lambda e: e.activation(out=rstd[:, :], in_=pl[:, :], func=AF.Ln, scale=1.0 / D, bias=EPS),
              reads=["pl"], writes=["rl"])
            C("act", lambda e: e.activation(out=rstd[:, :], in_=rstd[:, :], func=AF.Exp, scale=-0.5),
              reads=["rl"], writes=["rl"])
            for kc in range(KC):
                C("dve", lambda e, kc=kc, cs=cs, gcol=gcol: e.scalar_tensor_tensor(
                    out=hT[:, kc, cs], in0=xT[:, kc, cs], scalar=gpre[:, gcol + kc:gcol + kc + 1], in1=rstd[:, :],
                    op0=ALU.mult, op1=ALU.mult),
                  reads=[("xT", kc, half), "rl"], writes=[("hT", kc, half)])

        cck, ccv = {}, {}
        for kt in range(nkv):
            w = wneed()
            st = kt % NST
            for half in range(2):
                r, rk = hrhs(half)
                b = proj(w, r, rk)
                C("dve", lambda e, b=b, st=st, half=half: e.tensor_copy(
                    out=KTst[st][:, half * 512:(half + 1) * 512], in_=pj[b][:, :]),
                  reads=[("pj", b)], writes=[("KTst", st, half)])
            dma("sp", send_k[l][kt * 128:(kt + 1) * 128, :], KTst[st][:, :], s_kst[st],
                reads=[("KTst", st, 0), ("KTst", st, 1)], writes=[("send_k", l)])
            if kt % 2 == 1:
                g = kt // 2
                kst_toks = [o for o in P.ops["sp"] if o.dsem in s_kst][-2:]
                cck[g] = P.op("pool", lambda e, l=l, g=g: e.collective_compute(
                    "AllGather", ALU.bypass, replica_groups=RG, ins=[send_k[l][g * 256:(g + 1) * 256, :]],
                    outs=[recv_k[l][g * 512:(g + 1) * 512, :]]),
                    dsem=s_cck, inc=1, extra=kst_toks, writes=[("recv_k", l, g)])

        for vt in range(nkv):
            w = wneed()
            st = vt % NST
            for half in range(2):
                r, rk = hrhs(half)
                b = proj(w, r, rk)
                C("dve", lambda e, b=b, st=st, half=half: e.tensor_copy(
                    out=KTst[st][:, half * 512:(half + 1) * 512], in_=pj[b][:, :]),
                  reads=[("pj", b)], writes=[("KTst", st, half)])
            for i in range(NSLOT):
                C("pe", lambda e, st=st, i=i: e.transpose(pl_bf[:, i, :], KTst[st][:, i * 128:(i + 1) * 128], ident_bf),
                  reads=[("KTst", st, i // 4)], writes=["pl"])
            C("act", lambda e, st=st: e.activation(out=Vst[st][:, :, :], in_=pl_bf[:, :, :], func=AF.Copy),
              reads=["pl"], writes=[("Vst", st)])
            dma("sp", send_v[l][vt * 128:(vt + 1) * 128, :], Vst[st][:, :, :].rearrange("p a f -> p (a f)"),
                s_vst[st], reads=[("Vst", st)], writes=[("send_v", l)])
            if vt % 2 == 1:
                g = vt // 2
                vst_toks = [o for o in P.ops["sp"] if o.dsem in s_vst][-2:]
                ccv[g] = P.op("pool", lambda e, l=l, g=g: e.collective_compute(
                    "AllGather", ALU.bypass, replica_groups=RG, ins=[send_v[l][g * 256:(g + 1) * 256, :]],
                    outs=[recv_v[l][g * 512:(g + 1) * 512, :]]),
                    dsem=s_ccv, inc=1, extra=vst_toks, writes=[("recv_v", l, g)])

        if kind == 2:
            w = wneed()
            for i in range(NSLOT):
                for kc in range(KC):
                    C("pe", lambda e, i=i, kc=kc, w=w: e.matmul(
                        pj[0][:, i * 32:(i + 1) * 32], lhsT=hT[:, kc, i * 128:(i + 1) * 128], rhs=wb[w][:, kc, 0:32],
                        start=(kc == 0), stop=(kc == KC - 1)),
                      reads=[("wb", w, 0), ("wb", w, 1), ("hT", kc, i // 4)], writes=[("pj", 0)])
            pjc["i"] = 1
            fl = f32a[0]
            C("dve", lambda e: e.tensor_tensor(out=fl[:, 0:256], in0=pj[0][:, 0:256], in1=bfb[:, :], op=ALU.add),
              reads=[("pj", 0)], writes=[("f32a", 0)])
            C("act", lambda e: e.activation(out=fl[:, 0:256], in_=fl[:, 0:256], func=AF.Exp, scale=-1.0),
              reads=[("f32a", 0)], writes=[("f32a", 0)])
            C("act", lambda e: e.activation(out=fl[:, 0:256], in_=fl[:, 0:256], func=AF.Ln, bias=1.0),
              reads=[("f32a", 0)], writes=[("f32a", 0)])
            C("dve", lambda e: e.tensor_single_scalar(out=lf_own[:, :, :].rearrange("p a h -> p (a h)"),
                                                      in_=fl[:, 0:256], scalar=-1.0, op=ALU.mult),
              reads=[("f32a", 0)], writes=["lf_own"])
            fst = dma("sp", send_f[:, :], lf_own[:, :, :].rearrange("p a h -> p (a h)"), s_f, reads=["lf_own"])
            ccf = P.op("pool", lambda e: e.collective_compute(
                "AllGather", ALU.bypass, replica_groups=RG, ins=[send_f[:, :]], outs=[recv_f[:, :]]),
                dsem=s_ccf, inc=1, extra=[fst])
            for r in range(2):
                dma("act", lf_full[:, r::2, :], recv_f[r * 128:(r + 1) * 128, :].rearrange("p (a h) -> p a h", a=8),
                    s_f2, writes=[("Sacc", 0)], extra=[ccf])
            lff = lf_full[:, :, :].rearrange("p a h -> p (a h)")
            Lt = [bfa[0], bfa[1], Saccb[0]]
            Lk = [("bfa", 0), ("bfa", 1), ("Saccb", 0)]
            C("dve", lambda e: e.tensor_copy(out=Lt[0][:, :], in_=lff), reads=[("Sacc", 0)], writes=[Lk[0]])
            C("dve", lambda e: e.tensor_tensor(out=f32a[0][:, :], in0=lff, in1=Lt[0][:, :], op=ALU.subtract),
              reads=[("Sacc", 0), Lk[0]], writes=[("f32a", 0)])
            C("dve", lambda e: e.tensor_copy(out=Lt[1][:, :], in_=f32a[0][:, :]), reads=[("f32a", 0)], writes=[Lk[1]])
            C("dve", lambda e: e.tensor_tensor(out=f32a[1][:, :], in0=f32a[0][:, :], in1=Lt[1][:, :], op=ALU.subtract),
              reads=[("f32a", 0), Lk[1]], writes=[("f32a", 1)])
            C("dve", lambda e: e.tensor_copy(out=Lt[2][:, :], in_=f32a[1][:, :]), reads=[("f32a", 1)], writes=[Lk[2]])
            for k in range(3):
                C("pe", lambda e, k=k: e.matmul(pj[1][:, :], lhsT=triup_bf, rhs=Lt[k][:, :], start=(k == 0), stop=(k == 2)),
                  reads=[Lk[k]], writes=[("pj", 1)])
            for k in range(3):
                C("pe", lambda e, k=k: e.matmul(pj[0][:, :], lhsT=ones_bf, rhs=Lt[k][:, :], start=(k == 0), stop=(k == 2)),
                  reads=[Lk[k]], writes=[("pj", 0)])
            pjc["i"] = 0
            C("dve", lambda e: e.tensor_copy(out=totb[:, :, :].rearrange("p a h -> p (a h)"), in_=pj[0][:, :]),
              reads=[("pj", 0)], writes=[("Sacc", 1)])
            C("dve", lambda e: e.memset(carry[:, 0, :], 0.0), writes=["rl"])
            for kb in range(1, 16):
                C("dve", lambda e, kb=kb: e.tensor_tensor(out=carry[:, kb, :], in0=carry[:, kb - 1, :],
                                                          in1=totb[:, kb - 1, :], op=ALU.add),
                  reads=["rl", ("Sacc", 1)], writes=["rl"])
            C("dve", lambda e: e.tensor_tensor(out=cum[:, :, :].rearrange("p a h -> p (a h)"), in0=pj[1][:, :],
                                               in1=carry[:, :, :].rearrange("p a h -> p (a h)"), op=ALU.add),
              reads=[("pj", 1), "rl"], writes=["cum"])
            C("dve", lambda e: e.tensor_single_scalar(out=ncum[:, :, :].rearrange("p a h -> p (a h)"),
                                                      in_=cum[:, :, :].rearrange("p a h -> p (a h)"),
                                                      scalar=-1.0, op=ALU.mult), reads=["cum"], writes=["ncum"])
            C("dve", lambda e: e.tensor_single_scalar(out=cq_tmp[:, :, :], in_=cum[:, 0::2, :], scalar=par[:, 0:1],
                                                      op=ALU.mult), reads=["cum"], writes=["zs"])
            C("dve", lambda e: e.scalar_tensor_tensor(out=cq_own[:, :, :], in0=cum[:, 1::2, :], scalar=par[:, 1:2],
                                                      in1=cq_tmp[:, :, :], op0=ALU.mult, op1=ALU.add),
              reads=["cum", "zs"], writes=["zs"])
            cqo = cq_own[:, :, :].rearrange("p a h -> p (a h)")
            cqt = cq_tmp[:, :, :].rearrange("p a h -> p (a h)")
            lfo = lf_own[:, :, :].rearrange("p a h -> p (a h)")
            C("dve", lambda e: e.tensor_copy(out=cqs[:, 0, :], in_=cqo), reads=["zs"], writes=[("cqs", 0)])
            C("dve", lambda e: e.tensor_tensor(out=cqt, in0=cqo, in1=cqs[:, 0, :], op=ALU.subtract),
              reads=["zs", ("cqs", 0)], writes=["zs"])
            C("dve", lambda e: e.tensor_copy(out=cqs[:, 1, :], in_=cqt), reads=["zs"], writes=[("cqs", 1)])
            C("dve", lambda e: e.tensor_tensor(out=lfo, in0=cqt, in1=cqs[:, 1, :], op=ALU.subtract),
              reads=["zs", ("cqs", 1)], writes=["lf_own"])
            C("dve", lambda e: e.tensor_copy(out=cqs[:, 2, :], in_=lfo), reads=["lf_own"], writes=[("cqs", 2)])

        zc = {"i": 0}

        def load_pair(hp, s):
            def trow(t, r):
                return (t // 2) * 512 + r * 256 + (t % 2) * 128
            tix = hp // 4 if kind == 0 else hp
            ck, cv = cck[nkv // 2 - 1], ccv[nkv // 2 - 1]
            last = None
            prev_k, prev_v = P.last_w.get(("Kpair", s)), P.last_w.get(("Vpair", s))
            rk, rv = P.readers.get(("Kpair", s), {}), P.readers.get(("Vpair", s), {})
            last = None
            for r in range(2):
                base = trow(tix, r)
                P.last_w[("Kpair", s)], P.readers[("Kpair", s)] = prev_k, dict(rk)
                dma("act", Kpair[s][:, r * T:(r + 1) * T], recv_k[l][base:base + 128, :],
                    s_kv[s], writes=[("Kpair", s)], extra=[ck])
                P.last_w[("Vpair", s)], P.readers[("Vpair", s)] = prev_v, dict(rv)
                last = dma("act", Vpair[s][:, r, :, :],
                           recv_v[l][base:base + 128, :].rearrange("p (a f) -> p a f", a=8),
                           s_kv[s], writes=[("Vpair", s)], extra=[cv])
            P.last_w[("Kpair", s)] = last
            P.last_w[("Vpair", s)] = last

        def qz_proj(hp, s):
            wq = wneed()
            for half in range(2):
                r, rk = hrhs(half)
                b = yield from proj_gen(wq, r, rk)
                cs = slice(half * 512, (half + 1) * 512)
                C("dve", lambda e, b=b, s=s, cs=cs: e.tensor_single_scalar(
                    out=QA[s][0:64, cs], in_=pj[b][0:64, :], scalar=0.125, op=ALU.mult),
                  reads=[("pj", b)], writes=[("QA", s, half)])
                C("dve", lambda e, b=b, s=s, cs=cs: e.tensor_single_scalar(
                    out=QB[s][64:128, cs], in_=pj[b][64:128, :], scalar=0.125, op=ALU.mult),
                  reads=[("pj", b)], writes=[("QB", s, half)])
                yield None
            wz = wneed()
            for half in range(2):
                r, rk = hrhs(half)
                b = yield from proj_gen(wz, r, rk)
                cs = slice(half * 512, (half + 1) * 512)
                C("act", lambda e, b=b: e.activation(out=zs[:, :], in_=pj[b][:, :], func=AF.Exp, scale=-1.0),
                  reads=[("pj", b)], writes=["zs"])
                C("act", lambda e: e.activation(out=zs[:, :], in_=zs[:, :], func=AF.Ln, bias=1.0),
                  reads=["zs"], writes=["zs"])
                C("act", lambda e: e.activation(out=zs[:, :], in_=zs[:, :], func=AF.Exp, scale=-1.0),
                  reads=["zs"], writes=["zs"])
                C("dve", lambda e, b=b, s=s, cs=cs: e.tensor_tensor(out=SZ[s][:, cs], in0=pj[b][:, :],
                                                                 in1=zs[:, :], op=ALU.mult),
                  reads=[("pj", b), "zs"], writes=[("SZ", s, half)])
                yield None

        def finalize(hp, s, j, normalize, sink_col=None):
            cs = slice(j * 512, (j + 1) * 512)
            if normalize:
                if sink_col is not None:
                    C("act", lambda e: e.activation(out=rl[:, :], in_=pl[:, :], func=AF.Ln,
                                                    bias=esink[:, sink_col:sink_col + 1]),
                      reads=["pl", "esink"], writes=["rl"])
                else:
                    C("act", lambda e: e.activation(out=rl[:, :], in_=pl[:, :], func=AF.Ln), reads=["pl"], writes=["rl"])
                C("act", lambda e: e.activation(out=rl[:, :], in_=rl[:, :], func=AF.Exp, scale=-1.0),
                  reads=["rl"], writes=["rl"])
            for X in range(2):
                rs = slice(X * 64, (X + 1) * 64)
                if normalize:
                    C("dve", lambda e, X=X, rs=rs: e.tensor_tensor(out=f32a[X][rs, :], in0=pso[X][rs, :], in1=rl[rs, :],
                                                                 op=ALU.mult),
                      reads=[("pso", X), "rl"], writes=[("f32a", X)])
                    C("dve", lambda e, X=X, rs=rs, cs=cs: e.tensor_tensor(out=GT[rs, hp, cs], in0=f32a[X][rs, :],
                                                                        in1=SZ[s][rs, cs], op=ALU.mult),
                      reads=[("f32a", X), ("SZ", s, j)], writes=[("GT", hp, j)])
                else:
                    C("dve", lambda e, X=X, rs=rs, cs=cs: e.tensor_tensor(out=GT[rs, hp, cs], in0=pso[X][rs, :],
                                                                        in1=SZ[s][rs, cs], op=ALU.mult),
                      reads=[("pso", X), ("SZ", s, j)], writes=[("GT", hp, j)])

        def run3(tiles, stageA, stageB, bg, bgn):
            gens = {}
            N = len(tiles)

            def A1(k):
                gens[k] = stageA(tiles[k])
                next(gens[k])

            def A2(k):
                for _ in gens.pop(k):
                    pass

            A1(0)
            if N > 1:
                A1(1)
            A2(0)
            for n in range(N):
                if n + 2 < N:
                    A1(n + 2)
                if n + 1 < N:
                    A2(n + 1)
                stageB(tiles[n])
                if bg is not None:
                    for _ in range(bgn):
                        next(bg, None)

        def kcols(kb):
            return (kb % 2) * T + (kb // 2) * 128

        def attention_sb(hp, s, bg=None, bgn=1):
            tiles = []
            for j in range(2):
                kbs = list(range(8 * j + 7, -1, -1))
                for n, kb in enumerate(kbs):
                    for X in range(2):
                        m = kb - 8 * j
                        c0 = 0 if m < 0 else max(0, (m - 1 + 1) // 2)
                        tiles.append(dict(j=j, X=X, kb=kb, m=m, c0=c0, first=(n == 0), last=(n == len(kbs) - 1)))

            def stageA(t):
                j, X, kb, m, c0 = t["j"], t["X"], t["kb"], t["m"], t["c0"]
                zb = zc["i"] % 3
                zc["i"] += 1
                par2 = zc["i"] % 2
                t["zb"] = zb
                a = X
                t["a"] = a
                cl = slice(c0 * 128, 512)
                qs = slice(j * 512 + c0 * 128, (j + 1) * 512)
                Q = QA[s] if X == 0 else QB[s]
                qk = ("QA" if X == 0 else "QB", s, j)
                kc0 = kcols(kb)
                diag = m >= 0
                C("pe", lambda e: e.matmul(pz[zb][:, cl], lhsT=Kpair[s][:, kc0:kc0 + 128], rhs=Q[:, qs], start=True,
                                           stop=not diag),
                  reads=[("Kpair", s), qk], writes=[("pz", zb)])
                if diag:
                    mc = slice(c0 * 128, (c0 + 1) * 128)
                    C("pe", lambda e: e.matmul(pz[zb][:, mc], lhsT=ident_bf, rhs=maskb[:, m % 2, :], start=False, stop=True),
                      reads=["maskb"], writes=[("pz", zb)])
                yield
                if t["first"]:
                    C("dve", lambda e, a=a: e.memset(Sacc[a][:, :], 0.0), writes=[("Sacc", a)])
                    C("dve", lambda e, a=a: e.memset(Saccb[a][:, :], 0.0), writes=[("Saccb", a)])
                eb = 0
                C("act", lambda e: e.activation(out=f32a[eb][:, cl], in_=pz[zb][:, cl], func=AF.Exp),
                  reads=[("pz", zb)], writes=[("f32a", eb)])
                sb_ = par2
                t["sb"] = sb_
                C("act", lambda e: e.activation(out=bfa[sb_][:, cl], in_=f32a[eb][:, cl], func=AF.Ln, bias=1.0),
                  reads=[("f32a", eb)], writes=[("bfa", sb_)])
                C("pe", lambda e: e.matmul(pz[zb][:, cl], lhsT=negtri_bf, rhs=bfa[sb_][:, cl], start=False,
                                           stop=t["first"], skip_group_check=True),
                  reads=[("bfa", sb_)], writes=[("pz", zb)])
                if not t["first"]:
                    C("pe", lambda e: e.matmul(pz[zb][:, cl], lhsT=negones_bf, rhs=Saccb[a][:, cl], start=False, stop=True,
                                               skip_group_check=True),
                      reads=[("Saccb", a)], writes=[("pz", zb)])
                if not t["last"]:
                    C("dve", lambda e: e.tensor_tensor(out=Sacc[a][:, cl], in0=Sacc[a][:, cl], in1=bfa[sb_][:, cl],
                                                       op=ALU.add),
                      reads=[("Sacc", a), ("bfa", sb_)], writes=[("Sacc", a)])
                    C("dve", lambda e: e.tensor_copy(out=Saccb[a][:, cl], in_=Sacc[a][:, cl]),
                      reads=[("Sacc", a)], writes=[("Saccb", a)])

            def stageB(t):
                j, X, kb, c0, zb = t["j"], t["X"], t["kb"], t["c0"], t["zb"]
                cl = slice(c0 * 128, 512)
                ab = t["sb"]
                C("act", lambda e: e.activation(out=bfp[ab][:, cl], in_=pz[zb][:, cl], func=AF.Exp),
                  reads=[("pz", zb)], writes=[("bfp", ab)])
                if t["first"]:
                    C("pe", lambda e: e.matmul(pso[X][:, :], lhsT=zeros_bf, rhs=ident4_flat, start=True, stop=False),
                      writes=[("pso", X)])
                C("pe", lambda e: e.matmul(pso[X][:, cl], lhsT=Vpair[s][:, kb % 2, kb // 2, :], rhs=bfp[ab][:, cl], start=False,
                                           stop=t["last"]),
                  reads=[("Vpair", s), ("bfp", ab)], writes=[("pso", X)])
                if t["last"] and X == 1:
                    finalize(hp, s, j, normalize=False)

            return tiles, stageA, stageB

        def attention_fox(hp, s, bg=None, bgn=1):
            tiles = []
            for j in range(2):
                kbs = list(range(8 * j + 7, -1, -1))
                for n, kb in enumerate(kbs):
                    for X in range(2):
                        m = kb - 8 * j
                        c0 = 0 if m < 0 else max(0, m // 2)
                        tiles.append(dict(j=j, X=X, kb=kb, m=m, c0=c0, first=(n == 0), last=(n == len(kbs) - 1)))

            def stageA(t):
                j, X, kb, m, c0 = t["j"], t["X"], t["kb"], t["m"], t["c0"]
                h = 2 * hp + X
                zb = zc["i"] % 3
                zc["i"] += 1
                par2 = zc["i"] % 2
                t["zb"] = zb
                cl = slice(c0 * 128, 512)
                qs = slice(j * 512 + c0 * 128, (j + 1) * 512)
                Q = QA[s] if X == 0 else QB[s]
                qk = ("QA" if X == 0 else "QB", s, j)
                kc0 = kcols(kb)
                diag = m >= 0
                C("pe", lambda e: e.matmul(pz[zb][:, cl], lhsT=Kpair[s][:, kc0:kc0 + 128], rhs=Q[:, qs], start=True,
                                           stop=not diag),
                  reads=[("Kpair", s), qk], writes=[("pz", zb)])
                if diag:
                    mc = slice(c0 * 128, (c0 + 1) * 128)
                    C("pe", lambda e: e.matmul(pz[zb][:, mc], lhsT=ident_bf, rhs=maskb[:, m % 2, :], start=False, stop=True),
                      reads=["maskb"], writes=[("pz", zb)])
                yield
                if t["first"]:
                    b = pjc["i"] % 2
                    pjc["i"] += 1
                    Dt = [bfa[0], bfa[1], Saccb[0]]
                    Dk = [("bfa", 0), ("bfa", 1), ("Saccb", 0)]
                    for k in range(3):
                        cv = cqs[:, k, :].rearrange("p (a h) -> p a h", a=8)[:, 4 * j:4 * j + 4, h:h + 1]
                        C("dve", lambda e, k=k, cv=cv: e.tensor_tensor(
                            out=Dt[k][:, :].rearrange("p (c t) -> p c t", c=4), in0=ident4_bf,
                            in1=cv.to_broadcast([128, 4, 128]), op=ALU.mult),
                          reads=[("cqs", k)], writes=[Dk[k]])
                    for k in range(3):
                        C("pe", lambda e, b=b, k=k: e.matmul(pj[b][:, :], lhsT=ones_bf, rhs=Dt[k][:, :], start=(k == 0),
                                                           stop=(k == 2)),
                          reads=[Dk[k]], writes=[("pj", b)])
                    C("dve", lambda e, b=b: e.tensor_copy(out=cqb[X][:, :], in_=pj[b][:, :]),
                      reads=[("pj", b)], writes=[("Sacc", X)])
                fa = par2
                t["fa"] = fa
                C("dve", lambda e: e.tensor_tensor(out=f32a[fa][:, cl], in0=pz[zb][:, cl], in1=cqb[X][:, cl], op=ALU.add),
                  reads=[("pz", zb), ("Sacc", X)], writes=[("f32a", fa)])
                C("act", lambda e: e.activation(out=bfp[fa][:, cl], in_=f32a[fa][:, cl], func=AF.Exp,
                                                bias=ncum[:, kb, h:h + 1]),
                  reads=[("f32a", fa), "ncum"], writes=[("bfp", fa)])

            def stageB(t):
                j, X, kb, c0, fa = t["j"], t["X"], t["kb"], t["c0"], t["fa"]
                cl = slice(c0 * 128, 512)
                if t["first"]:
                    C("pe", lambda e: e.matmul(pso[X][:, :], lhsT=zeros_bf, rhs=ident4_flat, start=True, stop=False),
                      writes=[("pso", X)])
                    if X == 0:
                        C("pe", lambda e: e.matmul(pl[:, :], lhsT=zeros_bf, rhs=ident4_flat, start=True, stop=False),
                          writes=["pl"])
                C("pe", lambda e: e.matmul(pso[X][:, cl], lhsT=Vpair[s][:, kb % 2, kb // 2, :], rhs=bfp[fa][:, cl], start=False,
                                           stop=t["last"]),
                  reads=[("Vpair", s), ("bfp", fa)], writes=[("pso", X)])
                C("pe", lambda e: e.matmul(pl[:, cl], lhsT=(oneslo_bf if X == 0 else oneshi_bf), rhs=bfp[fa][:, cl],
                                           start=False, stop=(t["last"] and X == 1)),
                  reads=[("bfp", fa)], writes=["pl"])
                if t["last"] and X == 1:
                    finalize(hp, s, j, normalize=True)

            return tiles, stageA, stageB

        def attention_swa(hp, s, bg=None, bgn=1):
            tiles = []
            for j in range(2):
                for c in range(4):
                    for X in range(2):
                        i = 4 * j + c
                        ms = [m for m in range(3) if not (i == 0 and m == 0)]
                        tiles.append(dict(j=j, X=X, c=c, i=i, ms=ms, first=(c == 0), last=(c == 3)))

            def stageA(t):
                j, X, c, i, ms = t["j"], t["X"], t["c"], t["i"], t["ms"]
                h = 2 * hp + X
                zb = zc["i"] % 3
                zc["i"] += 1
                par2 = zc["i"] % 2
                t["zb"] = zb
                Q = QA[s] if X == 0 else QB[s]
                qk = ("QA" if X == 0 else "QB", s, j)
                qs = slice(i * 128, (i + 1) * 128)
                for n, m in enumerate(ms):
                    kb = 2 * i - 1 + m
                    kc0 = kcols(kb)
                    C("pe", lambda e, m=m, kc0=kc0, n=n: e.matmul(
                        pz[zb][:, m * 128:(m + 1) * 128], lhsT=Kpair[s][:, kc0:kc0 + 128], rhs=Q[:, qs],
                        start=(n == 0), stop=(n == len(ms) - 1)),
                      reads=[("Kpair", s), qk], writes=[("pz", zb)])
                yield
                ml = slice(ms[0] * 128, 384)
                fa = par2
                t["fa"] = fa
                C("dve", lambda e: e.scalar_tensor_tensor(out=f32a[fa][:, ml], in0=distA[:, ml], scalar=-slopes[h],
                                                          in1=pz[zb][:, ml], op0=ALU.mult, op1=ALU.add),
                  reads=[("pz", zb)], writes=[("f32a", fa)])
                C("act", lambda e: e.activation(out=bfp[fa][:, ml], in_=f32a[fa][:, ml], func=AF.Exp),
                  reads=[("f32a", fa)], writes=[("bfp", fa)])

            def stageB(t):
                j, X, c, i, ms, fa = t["j"], t["X"], t["c"], t["i"], t["ms"], t["fa"]
                oc = slice(c * 128, (c + 1) * 128)
                for n, m in enumerate(ms):
                    kb = 2 * i - 1 + m
                    C("pe", lambda e, m=m, kb=kb, n=n: e.matmul(
                        pso[X][:, oc], lhsT=Vpair[s][:, kb % 2, kb // 2, :], rhs=bfp[fa][:, m * 128:(m + 1) * 128],
                        start=(t["first"] and n == 0), stop=(t["last"] and n == len(ms) - 1)),
                      reads=[("Vpair", s), ("bfp", fa)], writes=[("pso", X)])
                for n, m in enumerate(ms):
                    C("pe", lambda e, m=m, n=n: e.matmul(
                        pl[:, oc], lhsT=(oneslo_bf if X == 0 else oneshi_bf), rhs=bfp[fa][:, m * 128:(m + 1) * 128],
                        start=(t["first"] and X == 0 and n == 0), stop=(t["last"] and X == 1 and n == len(ms) - 1)),
                      reads=[("bfp", fa)], writes=["pl"])
                if t["last"] and X == 1:
                    finalize(hp, s, j, normalize=True, sink_col=na_seen * 16 + hp)

            return tiles, stageA, stageB

        afn = {0: attention_swa, 1: attention_sb, 2: attention_fox}[kind]
        bgn = 3 if kind == 0 else 1
        items = []
        for hp in range(16):
            tl, sA, sB = afn(hp, hp % NBP)
            items += [(hp, t, sA, sB) for t in tl]
        gens, bgens, started = {}, {}, set()

        def start_pair(hp):
            if hp < 16 and hp not in started:
                started.add(hp)
                load_pair(hp, hp % NBP)
                bgens[hp] = qz_proj(hp, hp % NBP)

        def ready_pair(hp):
            start_pair(hp)
            g = bgens.pop(hp, None)
            if g is not None:
                for _ in g:
                    pass

        def A1(k):
            hp, t, sA, sB = items[k]
            ready_pair(hp)
            gens[k] = sA(t)
            next(gens[k])

        def A2(k):
            for _ in gens.pop(k):
                pass

        NI = len(items)
        A1(0)
        A1(1)
        A2(0)
        for n in range(NI):
            if n + 2 < NI:
                A1(n + 2)
            if n + 1 < NI:
                A2(n + 1)
            hp, t, sA, sB = items[n]
            sB(t)
            start_pair(hp + 1)
            g = bgens.get(hp + 1)
            if g is not None:
                for _ in range(bgn):
                    next(g, None)

        for half in range(2):
            cs = slice(half * 512, (half + 1) * 512)
            for m in range(KC):
                w = wneed()
                b = proj(w, (lambda kc, cs=cs: GT[:, kc, cs]), (lambda kc, half=half: [("GT", kc, half)]))
                C("act", lambda e, b=b, m=m: e.activation(out=ytmp[:, m, :], in_=pj[b][:, :], func=AF.Copy),
                  reads=[("pj", b)], writes=[("hT", m, 0), ("hT", m, 1)])
                sq = sqb[m % 2]
                C("act", lambda e, b=b, sq=sq: e.activation(out=sq[:, :], in_=pj[b][:, :], func=AF.Square),
                  reads=[("pj", b)], writes=[("bfa", m % 2)])
                C("pe", lambda e, m=m, sq=sq: e.matmul(pl[:, :], lhsT=ones_bf, rhs=sq[:, :], start=(m == 0),
                                                     stop=(m == KC - 1)),
                  reads=[("bfa", m % 2)], writes=["pl"])
            C("act", lambda e: e.activation(out=rstd[:, :], in_=pl[:, :], func=AF.Ln, scale=1.0 / D, bias=EPS),
              reads=["pl"], writes=["rl"])
            C("act", lambda e: e.activation(out=rstd[:, :], in_=rstd[:, :], func=AF.Exp, scale=-0.5),
              reads=["rl"], writes=["rl"])
            for m in range(KC):
                fa = m % 2
                C("dve", lambda e, m=m, fa=fa, gcol=gcol: e.scalar_tensor_tensor(
                    out=f32a[fa][:, :], in0=ytmp[:, m, :], scalar=gpost[:, gcol + m:gcol + m + 1], in1=rstd[:, :],
                    op0=ALU.mult, op1=ALU.mult),
                  reads=[("hT", m, 0), ("hT", m, 1), "rl"], writes=[("f32a", fa)])
                C("dve", lambda e, m=m, fa=fa, cs=cs: e.tensor_tensor(out=xT[:, m, cs], in0=xT[:, m, cs],
                                                                     in1=f32a[fa][:, :], op=ALU.add),
                  reads=[("xT", m, half), ("f32a", fa)], writes=[("xT", m, half)])
        if kind == 0:
            na_seen += 1

    for q in range(4):
        dma("sp", out_d[:, 4 * q * T:(4 * q + 4) * T].rearrange("p (a t) -> p a t", a=4), xT[:, 4 * q:4 * q + 4, :],
            s_out, reads=[("xT", kc, hf) for kc in range(4 * q, 4 * q + 4) for hf in range(2)])
    P.emit({"sp": [s_out]})
    es.close()
    return nc


def _tok_index(par):
    tl = np.arange(T)
    return (2 * (tl // 128) + par) * 128 + (tl % 128)


def _consts(par):
    bf = ml_dtypes.bfloat16
    j = np.arange(128)[:, None]
    s = np.arange(128)[None, :]
    ident = (j == s).astype(np.float32)
    ones = np.ones((128, 128), np.float32)
    negtri = -(j >= s).astype(np.float32)
    oneslo = np.zeros((128, 128), np.float32)
    oneslo[:, :64] = 1.0
    oneshi = np.zeros((128, 128), np.float32)
    oneshi[:, 64:] = 1.0
    triup = (j <= s).astype(np.float32)
    cbf = np.stack([ident, ones, negtri, -ones, oneslo, oneshi, triup, ident, ident, ident, ident, 0.0 * ident],
                   axis=1).reshape(128, 12 * 128).astype(bf)

    def zone_mask(strict):
        sl = np.arange(128)[:, None]
        tl = np.arange(128)[None, :]
        tri = np.where((sl < tl) if strict else (sl <= tl), 0.0, NEGBIG).astype(np.float32)
        zero = np.zeros((128, 128), np.float32)
        full = np.full((128, 128), NEGBIG, np.float32)
        ev, od = (tri, full) if par == 0 else (zero, tri)
        return np.stack([ev, od], axis=1).reshape(128, 2 * 128).astype(bf)

    maskSB = zone_mask(True)
    maskFX = zone_mask(False)
    dist = np.zeros((128, 3, 128), np.float32)
    sl = np.arange(128)[:, None]
    tl = np.arange(128)[None, :]
    for m in range(3):
        dd = (par + 1 - m) * 128 + tl - sl
        ok = (dd >= 0) & (dd < 128)
        dist[:, m, :] = np.where(ok, dd, 1.0e7)
    distA = dist.reshape(128, 384).astype(np.float32)
    parv = np.zeros((128, 2), np.float32)
    parv[:, 0] = 1.0 - par
    parv[:, 1] = float(par)
    return dict(cbf=cbf, maskSB=maskSB, maskFX=maskFX, distA=distA, par=parv)


def _weights_dev(l, w_in, w_out):
    plan = layer_tile_plan(l)
    out = np.zeros((len(plan), 2, 128, KC * 64), np.float32)
    for t, (kind, idx) in enumerate(plan):
        cols = tile_cols(l, kind, idx)
        n = len(cols)
        src = w_out if kind == "o" else w_in
        blk = np.zeros((128, KC, 128), np.float32)
        blk[:, :, :n] = src[:, cols].reshape(KC, 128, n).transpose(1, 0, 2)
        for hf in range(2):
            out[t, hf] = blk[:, :, hf * 64:(hf + 1) * 64].reshape(128, KC * 64)
    return out


def _vec_layout(v):
    return np.ascontiguousarray(v.reshape(DEPTH, KC, 128).transpose(2, 0, 1).reshape(128, DEPTH * KC)).astype(np.float32)


def _run(layers, x, g_pre, g_post, w_in_a, w_out_a, sinks_a, w_in_b, w_out_b, w_in_c, b_f_c, w_out_c):
    x = np.asarray(x, np.float32)
    nc = build_program(layers)
    wins = {0: (w_in_a[0], w_out_a[0]), 1: (w_in_b[0], w_out_b[0]), 2: (w_in_c[0], w_out_c[0]), 3: (w_in_a[1], w_out_a[1])}
    wdev = {l: _weights_dev(l, np.asarray(wins[l][0], np.float32), np.asarray(wins[l][1], np.float32)) for l in layers}
    gpre = _vec_layout(np.asarray(g_pre, np.float32))
    gpost = _vec_layout(np.asarray(g_post, np.float32))
    sk = np.zeros((128, 32), np.float32)
    sa = np.asarray(sinks_a, np.float32)
    for n in range(2):
        for hp in range(16):
            sk[:64, n * 16 + hp] = sa[n, 2 * hp]
            sk[64:, n * 16 + hp] = sa[n, 2 * hp + 1]
    bfb = np.tile(np.asarray(b_f_c, np.float32)[0][None, None, :], (128, 8, 1)).reshape(128, 256)
    consts = [_consts(0), _consts(1)]
    in_maps = []
    for c in range(8):
        b, par = c // 2, c % 2
        tok = _tok_index(par)
        xs = x[b][tok]
        xT = np.ascontiguousarray(xs.reshape(T, KC, 128).transpose(2, 1, 0)).reshape(128, KC * T)
        m = dict(xT=xT, gpre=gpre, gpost=gpost, sinks=sk, bfb=bfb)
        m.update(consts[par])
        for l in layers:
            m[f"w{l}"] = wdev[l]
        in_maps.append(m)
    res = run_bass_kernel_spmd(nc, in_maps, core_ids=list(range(8)))
    out = np.empty_like(x)
    for c in range(8):
        b, par = c // 2, c % 2
        tok = _tok_index(par)
        oT = np.asarray(res.results[c]["outT"], np.float32).reshape(128, KC, T)
        out[b][tok] = oT.transpose(2, 1, 0).reshape(T, D)
    return out


def kernel(x, g_pre, g_post, w_in_a, w_out_a, sinks_a, w_in_b, w_out_b, w_in_c, b_f_c, w_out_c):
    return _run([0, 1, 2, 3], x, g_pre, g_post, w_in_a, w_out_a, sinks_a, w_in_b, w_out_b, w_in_c, b_f_c, w_out_c)
```

```python
import numpy as np
import ml_dtypes
from contextlib import ExitStack

import concourse.bass as bass
import concourse.mybir as mybir
from concourse.bass_utils import run_bass_kernel_spmd

F32 = mybir.dt.float32
BF16 = mybir.dt.bfloat16
AF = mybir.ActivationFunctionType
ALU = mybir.AluOpType

D = 2048
KC = 16
T = 1024
NSLOT = 8
H = 32
HD = 64
DEPTH = 4
EPS = 1e-6
NEGBIG = -30000.0
KINDS = [0, 1, 2, 0]
KIDX = [0, 0, 0, 1]
ENGS = ("pe", "act", "dve", "pool", "sp")
STRICT_SAME_ENGINE = True


def _slopes():
    return [float(np.float32(2.0) ** np.float32(-8.0 * (h + 1) / H)) for h in range(H)]


class DSem:
    def __init__(self, h):
        self.h = h
        self.n = 0


class Op:
    __slots__ = ("eng", "fn", "deps", "sig", "ordv", "kind", "dsem", "dval", "inc")

    def __init__(self, eng, fn, kind):
        self.eng = eng
        self.fn = fn
        self.kind = kind
        self.deps = []
        self.sig = False
        self.ordv = 0
        self.dsem = None
        self.dval = 0
        self.inc = 0


class Prog:
    def __init__(self, nc, es):
        self.nc = nc
        self.es = es
        self.ops = {e: [] for e in ENGS}
        self.last_w = {}
        self.readers = {}
        self.psem = {e: es.enter_context(nc.semaphore("prog_" + e)) for e in ENGS}
        self.nsem = 0

    def dsem(self, name):
        self.nsem += 1
        return DSem(self.es.enter_context(self.nc.semaphore(name)))

    def op(self, eng, fn, reads=(), writes=(), dsem=None, inc=16, extra=()):
        kind = "d" if dsem is not None else "c"
        o = Op(eng, fn, kind)
        deps = {}
        raw = set()
        for k in reads:
            w = self.last_w.get(k)
            if w is not None:
                deps[id(w)] = w
                raw.add(id(w))
        for k in writes:
            w = self.last_w.get(k)
            if w is not None:
                deps[id(w)] = w
            rd = self.readers.get(k)
            if rd:
                for r in rd.values():
                    if isinstance(r, list):
                        for rr in r:
                            deps[id(rr)] = rr
                    else:
                        deps[id(r)] = r
        for d in extra:
            deps[id(d)] = d
            raw.add(id(d))
        for i, d in deps.items():
            if d is o:
                continue
            if kind == "c" and d.kind == "c" and d.eng == eng:
                if eng == "pe" or (STRICT_SAME_ENGINE is False and i not in raw):
                    continue
            o.deps.append(d)
            if d.kind == "c":
                d.sig = True
        if kind == "d":
            dsem.n += inc
            o.dsem = dsem
            o.dval = dsem.n
            o.inc = inc
        for k in writes:
            self.last_w[k] = o
            self.readers[k] = {}
        for k in reads:
            rd = self.readers.setdefault(k, {})
            if kind == "d":
                rd.setdefault("dma", []).append(o)
            else:
                rd[eng] = o
        self.ops[eng].append(o)
        return o

    def emit(self, final_waits):
        nc = self.nc
        for e in ENGS:
            c = 0
            for o in self.ops[e]:
                if o.kind == "c" and o.sig:
                    c += 1
                    o.ordv = c
        psem = self.psem

        def run(eng_name, eng):
            waited = {}
            for o in self.ops[eng_name]:
                needs = {}
                for d in o.deps:
                    if d.kind == "c":
                        s, v = psem[d.eng], d.ordv
                    else:
                        s, v = d.dsem.h, d.dval
                    key = id(s)
                    if key not in needs or needs[key][1] < v:
                        needs[key] = (s, v)
                for key, (s, v) in needs.items():
                    if waited.get(key, 0) < v:
                        eng.wait_ge(s, v)
                        waited[key] = v
                ins = o.fn(eng)
                if o.kind == "c":
                    if o.sig:
                        ins.then_inc(psem[eng_name], 1)
                else:
                    ins.then_inc(o.dsem.h, o.inc)
            for ds in final_waits.get(eng_name, ()):
                eng.wait_ge(ds.h, ds.n)

        with nc.Block() as block:
            @block.tensor
            def _(e):
                run("pe", e)

            @block.scalar
            def _(e):
                run("act", e)

            @block.vector
            def _(e):
                run("dve", e)

            @block.gpsimd
            def _(e):
                run("pool", e)

            @block.sync
            def _(e):
                run("sp", e)


def layer_tile_plan(l):
    kind = KINDS[l]
    nkv = 4 if kind == 0 else 16
    plan = [("k", i) for i in range(nkv)] + [("v", i) for i in range(nkv)]
    if kind == 2:
        plan.append(("f", 0))
    for hp in range(16):
        plan.append(("q", hp))
        plan.append(("z", hp))
    plan += [("o", m) for m in range(16)]
    return plan


def tile_cols(l, kind, idx):
    k = KINDS[l]
    if kind == "o":
        return np.arange(idx * 128, (idx + 1) * 128)
    if k == 0:
        offs = {"q": 0, "k": 2048, "v": 2304, "z": 2560}
        if kind in ("k", "v"):
            c = offs[kind] + idx * 64 + np.arange(64)
            return np.concatenate([c, c])
    else:
        offs = {"q": 0, "k": 2048, "v": 4096, "z": 6144, "f": 8192}
    if kind == "f":
        return offs["f"] + np.arange(32)
    return offs[kind] + idx * 128 + np.arange(128)


def build_program(layers):
    nc = bass.Bass("TRN2", target_bir_lowering=False)
    es = ExitStack()
    P = Prog(nc, es)

    def dram_in(name, shape, dt=F32):
        return nc.dram_tensor(name, list(shape), dt, kind="ExternalInput").ap()

    xT_d = dram_in("xT", [128, KC * T])
    out_d = nc.dram_tensor("outT", [128, KC * T], F32, kind="ExternalOutput").ap()
    gpre_d = dram_in("gpre", [128, DEPTH * KC])
    gpost_d = dram_in("gpost", [128, DEPTH * KC])
    w_d = {}
    plans = {}
    for l in layers:
        plans[l] = layer_tile_plan(l)
        w_d[l] = dram_in(f"w{l}", [len(plans[l]), 2, 128, KC * 64])
    cbf_d = dram_in("cbf", [128, 12 * 128], BF16)
    maskSB_d = dram_in("maskSB", [128, 2 * 128], BF16)
    maskFX_d = dram_in("maskFX", [128, 2 * 128], BF16)
    distA_d = dram_in("distA", [128, 3 * 128])
    sinks_d = dram_in("sinks", [128, 2 * 16])
    bfb_d = dram_in("bfb", [128, 8 * 32])
    par_d = dram_in("par", [128, 2])

    send_k, recv_k, send_v, recv_v = {}, {}, {}, {}
    for l in layers:
        n = 4 if KINDS[l] == 0 else 16
        send_k[l] = nc.dram_tensor(f"send_k{l}", [n * 128, T], BF16)
        recv_k[l] = nc.dram_tensor(f"recv_k{l}", [2 * n * 128, T], BF16)
        send_v[l] = nc.dram_tensor(f"send_v{l}", [n * 128, T], BF16)
        recv_v[l] = nc.dram_tensor(f"recv_v{l}", [2 * n * 128, T], BF16)
    send_f = nc.dram_tensor("send_f", [128, 256], F32)
    recv_f = nc.dram_tensor("recv_f", [256, 256], F32)

    sbtot = {"b": 0}

    def sb(name, shape, dt):
        sbtot["b"] += int(np.prod(shape[1:])) * (4 if dt == F32 else 2)
        return es.enter_context(nc.sbuf_tensor("s_" + name, list(shape), dt))

    def ps(name, shape, dt=F32):
        return es.enter_context(nc.psum_tensor("p_" + name, list(shape), dt))

    xT = sb("xT", [128, KC, T], F32)
    hT = sb("hT", [128, KC, T], BF16)
    ytmp = hT.bitcast(F32).reshape([128, KC, 512])
    GT = sb("GT", [128, KC, T], BF16)
    stage = [sb(f"stage{i}", [128, KC, 64], F32) for i in range(2)]
    wb = [sb(f"wb{i}", [128, KC, 128], BF16) for i in range(2)]
    NBP = 2
    Kpair = [sb(f"Kpair{i}", [128, 2 * T], BF16) for i in range(NBP)]
    Vpair = [sb(f"Vpair{i}", [128, 2, 8, 128], BF16) for i in range(NBP)]
    QA = [sb(f"QA{i}", [128, T], BF16) for i in range(NBP)]
    QB = [sb(f"QB{i}", [128, T], BF16) for i in range(NBP)]
    SZ = [sb(f"SZ{i}", [128, T], BF16) for i in range(NBP)]
    maskb = sb("maskb", [128, 2, 128], BF16)
    cbf = sb("cbf", [128, 12, 128], BF16)
    gpre = sb("gpre", [128, DEPTH * KC], F32)
    gpost = sb("gpost", [128, DEPTH * KC], F32)
    distA = sb("distA", [128, 3 * 128], F32)
    sinks = sb("sinks", [128, 32], F32)
    esink = sb("esink", [128, 32], F32)
    bfb = sb("bfb", [128, 256], F32)
    par = sb("par", [128, 2], F32)
    NST = 1
    KTst = [sb(f"KTst{i}", [128, T], BF16) for i in range(NST)]
    Vst = [sb(f"Vst{i}", [128, 8, 128], BF16) for i in range(NST)]
    f32a = [sb(f"f32a{i}", [128, 512], F32) for i in range(2)]
    bfa = [sb(f"bfa{i}", [128, 512], BF16) for i in range(2)]
    bfp = [sb(f"bfp{i}", [128, 512], BF16) for i in range(2)]
    Sacc = [sb(f"Sacc{i}", [128, 512], F32) for i in range(2)]
    Saccb = [sb(f"Saccb{i}", [128, 512], BF16) for i in range(2)]
    cqb = Sacc
    rl = sb("rl", [128, 512], F32)
    rstd = rl
    zs = sb("zs", [128, 512], F32)
    zsv = zs.reshape([128, 16, 32])
    cq_own = zsv[:, 0:8, :]
    cq_tmp = zsv[:, 8:16, :]
    sqb = bfa
    lf_own = sb("lf_own", [128, 8, 32], F32)
    lf_full = Sacc[0].reshape([128, 16, 32])
    totb = Sacc[1].reshape([128, 16, 32])
    carry = rl.reshape([128, 16, 32])
    cum = sb("cum", [128, 16, 32], F32)
    ncum = sb("ncum", [128, 16, 32], F32)

    print("SBUF bytes/partition:", sbtot["b"])
    pj = [ps(f"pj{i}", [128, 512]) for i in range(2)]
    pz = [ps(f"pz{i}", [128, 512]) for i in range(3)]
    pso = [ps(f"pso{i}", [128, 512]) for i in range(2)]
    pl = ps("pl", [128, 512])
    pl_bf = pl.bitcast(BF16).reshape([128, 8, 128])

    ident_bf, ones_bf, negtri_bf, negones_bf, oneslo_bf, oneshi_bf = [cbf[:, i, :] for i in range(6)]
    triup_bf = cbf[:, 6, :]
    ident4_bf = cbf[:, 7:11, :]
    zeros_bf = cbf[:, 11, :]
    ident4_flat = cbf[:, 7:11, :].rearrange("p a t -> p (a t)")
    cqs = sb("cqs", [128, 3, 256], BF16)

    s_init = P.dsem("s_init")
    s_st = [P.dsem(f"s_st{i}") for i in range(2)]
    s_kst = [P.dsem(f"s_kst{i}") for i in range(2)]
    s_vst = [P.dsem(f"s_vst{i}") for i in range(2)]
    s_kv = [P.dsem(f"s_kv{i}") for i in range(2)]
    s_cck = P.dsem("s_cck")
    s_ccv = P.dsem("s_ccv")
    s_ccf = P.dsem("s_ccf")
    s_f = P.dsem("s_f")
    s_f2 = P.dsem("s_f2")
    s_mask = P.dsem("s_mask")
    s_out = P.dsem("s_out")

    RG = [[0, 1], [2, 3], [4, 5], [6, 7]]

    init_keys = ["cbf", "cf32", "gpre", "gpost", "distA", "sinks", "bfb", "par"]

    def dma(eng, out, in_, dsem, reads=(), writes=(), extra=()):
        return P.op(eng, lambda e, o=out, i=in_: e.dma_start(out=o, in_=i), reads=reads, writes=writes,
                    dsem=dsem, extra=extra)

    for q in range(4):
        dma("sp", xT[:, 4 * q:4 * q + 4, :], xT_d[:, 4 * q * T:(4 * q + 4) * T].rearrange("p (a t) -> p a t", a=4),
            s_init, writes=[("xT", kc, hf) for kc in range(4 * q, 4 * q + 4) for hf in range(2)])
    for name, sbt, dr in (("cbf", cbf, cbf_d),):
        dma("sp", sbt[:, :, :], dr.rearrange("p (a t) -> p a t", t=128), s_init, writes=[name])
    for name, sbt, dr in (("gpre", gpre, gpre_d), ("gpost", gpost, gpost_d), ("distA", distA, distA_d),
                          ("sinks", sinks, sinks_d), ("bfb", bfb, bfb_d), ("par", par, par_d)):
        dma("sp", sbt[:, :], dr, s_init, writes=[name])
    init_tok = P.ops["sp"][-1]

    first_done = set()

    def C(eng, fn, reads=(), writes=(), extra=()):
        ex = list(extra)
        if eng not in first_done:
            first_done.add(eng)
            ex.append(init_tok)
        return P.op(eng, fn, reads=reads, writes=writes, extra=ex)

    for i in range(NBP):
        C("pool", lambda e, t=QA[i]: e.memset(t[:, :], 0.0), writes=[("QA", i, 0), ("QA", i, 1)])
        C("pool", lambda e, t=QB[i]: e.memset(t[:, :], 0.0), writes=[("QB", i, 0), ("QB", i, 1)])
    C("act", lambda e: e.activation(out=esink[:, :], in_=sinks[:, :], func=AF.Exp), writes=["esink"])

    wlist = []
    for l in layers:
        plan = plans[l]
        pos = {kv: i for i, kv in enumerate(plan)}
        kind = KINDS[l]
        nkv = 4 if kind == 0 else 16
        order = [("k", i) for i in range(nkv)] + [("v", i) for i in range(nkv)]
        if kind == 2:
            order.append(("f", 0))
        for hp in range(16):
            order += [("q", hp), ("z", hp)]
        order += [("o", m) for m in range(16)] * 2
        for kv in order:
            wlist.append((l, pos[kv]))
    wstate = {"fetched": 0}

    def wfetch(i):
        l, t = wlist[i]
        s = i % 2
        tk = plans[l][t][0]
        for hf in range(2):
            dma("sp", stage[hf][:, :, :], w_d[l][t, hf].rearrange("p (a n) -> p a n", a=KC), s_st[hf],
                writes=[("stage", hf)])
            if tk in ("q", "z"):
                C("pool", lambda e, s=s, hf=hf: e.tensor_copy(out=wb[s][:, :, hf * 64:(hf + 1) * 64],
                                                              in_=stage[hf][:, :, :]),
                  reads=[("stage", hf)], writes=[("wb", s, hf)])
            elif hf == 0:
                C("dve", lambda e, s=s, hf=hf: e.tensor_copy(out=wb[s][:, :, hf * 64:(hf + 1) * 64],
                                                             in_=stage[hf][:, :, :]),
                  reads=[("stage", hf)], writes=[("wb", s, hf)])
            else:
                C("act", lambda e, s=s, hf=hf: e.activation(out=wb[s][:, :, hf * 64:(hf + 1) * 64],
                                                            in_=stage[hf][:, :, :], func=AF.Copy),
                  reads=[("stage", hf)], writes=[("wb", s, hf)])

    wcur = {"i": 0}

    def wneed():
        i = wcur["i"]
        wcur["i"] += 1
        while wstate["fetched"] < min(len(wlist), i + 2):
            wfetch(wstate["fetched"])
            wstate["fetched"] += 1
        return i % 2

    pjc = {"i": 0}

    def proj(wslot, rhs_of_kc, rhs_keys_of_kc, ncol=128, half=None):
        b = pjc["i"] % 2
        pjc["i"] += 1
        for kc in range(KC):
            rhs = rhs_of_kc(kc)
            C("pe", lambda e, b=b, kc=kc, w=wslot, rhs=rhs: e.matmul(pj[b][0:ncol, :], lhsT=wb[w][:, kc, 0:ncol],
                                                                 rhs=rhs, start=(kc == 0), stop=(kc == KC - 1)),
              reads=[("wb", wslot, 0), ("wb", wslot, 1)] + rhs_keys_of_kc(kc), writes=[("pj", b)])
        return b

    def proj_gen(wslot, rhs_of_kc, rhs_keys_of_kc, ncol=128, chunk=4):
        b = pjc["i"] % 2
        pjc["i"] += 1
        for kc in range(KC):
            rhs = rhs_of_kc(kc)
            C("pe", lambda e, b=b, kc=kc, w=wslot, rhs=rhs: e.matmul(pj[b][0:ncol, :], lhsT=wb[w][:, kc, 0:ncol],
                                                                 rhs=rhs, start=(kc == 0), stop=(kc == KC - 1)),
              reads=[("wb", wslot, 0), ("wb", wslot, 1)] + rhs_keys_of_kc(kc), writes=[("pj", b)])
            if kc % chunk == chunk - 1 and kc != KC - 1:
                yield None
        return b

    def hrhs(half):
        return (lambda kc: hT[:, kc, half * 512:(half + 1) * 512]), (lambda kc: [("hT", kc, half)])

    def rmsnorm_stats(src_of, src_keys, n, accum_bank_key, accum_ap):
        pass

    slopes = _slopes()
    na_seen = 0
    for l in layers:
        kind = KINDS[l]
        nkv = 4 if kind == 0 else 16
        gcol = l * KC

        if kind in (1, 2):
            src = maskSB_d if kind == 1 else maskFX_d
            dma("sp", maskb[:, :, :], src.rearrange("p (a t) -> p a t", a=2), s_mask, writes=["maskb"])

        for half in range(2):
            cs = slice(half * 512, (half + 1) * 512)
            for kc in range(KC):
                sq = sqb[kc % 2]
                C("act", lambda e, kc=kc, cs=cs, sq=sq: e.activation(out=sq[:, :], in_=xT[:, kc, cs], func=AF.Square),
                  reads=[("xT", kc, half)], writes=[("bfa", kc % 2)])
                C("pe", lambda e, kc=kc, sq=sq: e.matmul(pl[:, :], lhsT=ones_bf, rhs=sq[:, :], start=(kc == 0),
                                                       stop=(kc == KC - 1)),
                  reads=[("bfa", kc % 2)], writes=["pl"])
            C("act", lambda e: e.activation(out=rstd[:, :], in_=pl[:, :], func=AF.Ln, scale=1.0 / D, bias=EPS),
              reads=["pl"], writes=["rl"])
            C("act", lambda e: e.activation(out=rstd[:, :], in_=rstd[:, :], func=AF.Exp, scale=-0.5),
              reads=["rl"], writes=["rl"])
            for kc in range(KC):
                C("dve", lambda e, kc=kc, cs=cs, gcol=gcol: e.scalar_tensor_tensor(
                    out=hT[:, kc, cs], in0=xT[:, kc, cs], scalar=gpre[:, gcol + kc:gcol + kc + 1], in1=rstd[:, :],
                    op0=ALU.mult, op1=ALU.mult),
                  reads=[("xT", kc, half), "rl"], writes=[("hT", kc, half)])

        cck, ccv = {}, {}
        for kt in range(nkv):
            w = wneed()
            st = kt % NST
            for half in range(2):
                r, rk = hrhs(half)
                b = proj(w, r, rk)
                C("dve", lambda e, b=b, st=st, half=half: e.tensor_copy(
                    out=KTst[st][:, half * 512:(half + 1) * 512], in_=pj[b][:, :]),
                  reads=[("pj", b)], writes=[("KTst", st, half)])
            dma("sp", send_k[l][kt * 128:(kt + 1) * 128, :], KTst[st][:, :], s_kst[st],
                reads=[("KTst", st, 0), ("KTst", st, 1)], writes=[("send_k", l)])
            if kt % 2 == 1:
                g = kt // 2
                kst_toks = [o for o in P.ops["sp"] if o.dsem in s_kst][-2:]
                cck[g] = P.op("pool", lambda e, l=l, g=g: e.collective_compute(
                    "AllGather", ALU.bypass, replica_groups=RG, ins=[send_k[l][g * 256:(g + 1) * 256, :]],
                    outs=[recv_k[l][g * 512:(g + 1) * 512, :]]),
                    dsem=s_cck, inc=1, extra=kst_toks, writes=[("recv_k", l, g)])

        for vt in range(nkv):
            w = wneed()
            st = vt % NST
            for half in range(2):
                r, rk = hrhs(half)
                b = proj(w, r, rk)
                C("dve", lambda e, b=b, st=st, half=half: e.tensor_copy(
                    out=KTst[st][:, half * 512:(half + 1) * 512], in_=pj[b][:, :]),
                  reads=[("pj", b)], writes=[("KTst", st, half)])
            for i in range(NSLOT):
                C("pe", lambda e, st=st, i=i: e.transpose(pl_bf[:, i, :], KTst[st][:, i * 128:(i + 1) * 128], ident_bf),
                  reads=[("KTst", st, i // 4)], writes=["pl"])
            C("act", lambda e, st=st: e.activation(out=Vst[st][:, :, :], in_=pl_bf[:, :, :], func=AF.Copy),
              reads=["pl"], writes=[("Vst", st)])
            dma("sp", send_v[l][vt * 128:(vt + 1) * 128, :], Vst[st][:, :, :].rearrange("p a f -> p (a f)"),
                s_vst[st], reads=[("Vst", st)], writes=[("send_v", l)])
            if vt % 2 == 1:
                g = vt // 2
                vst_toks = [o for o in P.ops["sp"] if o.dsem in s_vst][-2:]
                ccv[g] = P.op("pool", lambda e, l=l, g=g: e.collective_compute(
                    "AllGather", ALU.bypass, replica_groups=RG, ins=[send_v[l][g * 256:(g + 1) * 256, :]],
                    outs=[recv_v[l][g * 512:(g + 1) * 512, :]]),
                    dsem=s_ccv, inc=1, extra=vst_toks, writes=[("recv_v", l, g)])

        if kind == 2:
            w = wneed()
            for i in range(NSLOT):
                for kc in range(KC):
                    C("pe", lambda e, i=i, kc=kc, w=w: e.matmul(
                        pj[0][:, i * 32:(i + 1) * 32], lhsT=hT[:, kc, i * 128:(i + 1) * 128], rhs=wb[w][:, kc, 0:32],
                        start=(kc == 0), stop=(kc == KC - 1)),
                      reads=[("wb", w, 0), ("wb", w, 1), ("hT", kc, i // 4)], writes=[("pj", 0)])
            pjc["i"] = 1
            fl = f32a[0]
            C("dve", lambda e: e.tensor_tensor(out=fl[:, 0:256], in0=pj[0][:, 0:256], in1=bfb[:, :], op=ALU.add),
              reads=[("pj", 0)], writes=[("f32a", 0)])
            C("act", lambda e: e.activation(out=fl[:, 0:256], in_=fl[:, 0:256], func=AF.Exp, scale=-1.0),
              reads=[("f32a", 0)], writes=[("f32a", 0)])
            C("act", lambda e: e.activation(out=fl[:, 0:256], in_=fl[:, 0:256], func=AF.Ln, bias=1.0),
              reads=[("f32a", 0)], writes=[("f32a", 0)])
            C("dve", lambda e: e.tensor_single_scalar(out=lf_own[:, :, :].rearrange("p a h -> p (a h)"),
                                                      in_=fl[:, 0:256], scalar=-1.0, op=ALU.mult),
              reads=[("f32a", 0)], writes=["lf_own"])
            fst = dma("sp", send_f[:, :], lf_own[:, :, :].rearrange("p a h -> p (a h)"), s_f, reads=["lf_own"])
            ccf = P.op("pool", lambda e: e.collective_compute(
                "AllGather", ALU.bypass, replica_groups=RG, ins=[send_f[:, :]], outs=[recv_f[:, :]]),
                dsem=s_ccf, inc=1, extra=[fst])
            for r in range(2):
                dma("act", lf_full[:, r::2, :], recv_f[r * 128:(r + 1) * 128, :].rearrange("p (a h) -> p a h", a=8),
                    s_f2, writes=[("Sacc", 0)], extra=[ccf])
            lff = lf_full[:, :, :].rearrange("p a h -> p (a h)")
            Lt = [bfa[0], bfa[1], Saccb[0]]
            Lk = [("bfa", 0), ("bfa", 1), ("Saccb", 0)]
            C("dve", lambda e: e.tensor_copy(out=Lt[0][:, :], in_=lff), reads=[("Sacc", 0)], writes=[Lk[0]])
            C("dve", lambda e: e.tensor_tensor(out=f32a[0][:, :], in0=lff, in1=Lt[0][:, :], op=ALU.subtract),
              reads=[("Sacc", 0), Lk[0]], writes=[("f32a", 0)])
            C("dve", lambda e: e.tensor_copy(out=Lt[1][:, :], in_=f32a[0][:, :]), reads=[("f32a", 0)], writes=[Lk[1]])
            C("dve", lambda e: e.tensor_tensor(out=f32a[1][:, :], in0=f32a[0][:, :], in1=Lt[1][:, :], op=ALU.subtract),
              reads=[("f32a", 0), Lk[1]], writes=[("f32a", 1)])
            C("dve", lambda e: e.tensor_copy(out=Lt[2][:, :], in_=f32a[1][:, :]), reads=[("f32a", 1)], writes=[Lk[2]])
            for k in range(3):
                C("pe", lambda e, k=k: e.matmul(pj[1][:, :], lhsT=triup_bf, rhs=Lt[k][:, :], start=(k == 0), stop=(k == 2)),
                  reads=[Lk[k]], writes=[("pj", 1)])
            for k in range(3):
                C("pe", lambda e, k=k: e.matmul(pj[0][:, :], lhsT=ones_bf, rhs=Lt[k][:, :], start=(k == 0), stop=(k == 2)),
                  reads=[Lk[k]], writes=[("pj", 0)])
            pjc["i"] = 0
            C("dve", lambda e: e.tensor_copy(out=totb[:, :, :].rearrange("p a h -> p (a h)"), in_=pj[0][:, :]),
              reads=[("pj", 0)], writes=[("Sacc", 1)])
            C("dve", lambda e: e.memset(carry[:, 0, :], 0.0), writes=["rl"])
            for kb in range(1, 16):
                C("dve", lambda e, kb=kb: e.tensor_tensor(out=carry[:, kb, :], in0=carry[:, kb - 1, :],
                                                          in1=totb[:, kb - 1, :], op=ALU.add),
                  reads=["rl", ("Sacc", 1)], writes=["rl"])
            C("dve", lambda e: e.tensor_tensor(out=cum[:, :, :].rearrange("p a h -> p (a h)"), in0=pj[1][:, :],
                                               in1=carry[:, :, :].rearrange("p a h -> p (a h)"), op=ALU.add),
              reads=[("pj", 1), "rl"], writes=["cum"])
            C("dve", lambda e: e.tensor_single_scalar(out=ncum[:, :, :].rearrange("p a h -> p (a h)"),
                                                      in_=cum[:, :, :].rearrange("p a h -> p (a h)"),
                                                      scalar=-1.0, op=ALU.mult), reads=["cum"], writes=["ncum"])
            C("dve", lambda e: e.tensor_single_scalar(out=cq_tmp[:, :, :], in_=cum[:, 0::2, :], scalar=par[:, 0:1],
                                                      op=ALU.mult), reads=["cum"], writes=["zs"])
            C("dve", lambda e: e.scalar_tensor_tensor(out=cq_own[:, :, :], in0=cum[:, 1::2, :], scalar=par[:, 1:2],
                                                      in1=cq_tmp[:, :, :], op0=ALU.mult, op1=ALU.add),
              reads=["cum", "zs"], writes=["zs"])
            cqo = cq_own[:, :, :].rearrange("p a h -> p (a h)")
            cqt = cq_tmp[:, :, :].rearrange("p a h -> p (a h)")
            lfo = lf_own[:, :, :].rearrange("p a h -> p (a h)")
            C("dve", lambda e: e.tensor_copy(out=cqs[:, 0, :], in_=cqo), reads=["zs"], writes=[("cqs", 0)])
            C("dve", lambda e: e.tensor_tensor(out=cqt, in0=cqo, in1=cqs[:, 0, :], op=ALU.subtract),
              reads=["zs", ("cqs", 0)], writes=["zs"])
            C("dve", lambda e: e.tensor_copy(out=cqs[:, 1, :], in_=cqt), reads=["zs"], writes=[("cqs", 1)])
            C("dve", lambda e: e.tensor_tensor(out=lfo, in0=cqt, in1=cqs[:, 1, :], op=ALU.subtract),
              reads=["zs", ("cqs", 1)], writes=["lf_own"])
            C("dve", lambda e: e.tensor_copy(out=cqs[:, 2, :], in_=lfo), reads=["lf_own"], writes=[("cqs", 2)])

        zc = {"i": 0}

        def load_pair(hp, s):
            def trow(t, r):
                return (t // 2) * 512 + r * 256 + (t % 2) * 128
            tix = hp // 4 if kind == 0 else hp
            ck, cv = cck[nkv // 2 - 1], ccv[nkv // 2 - 1]
            last = None
            prev_k, prev_v = P.last_w.get(("Kpair", s)), P.last_w.get(("Vpair", s))
            rk, rv = P.readers.get(("Kpair", s), {}), P.readers.get(("Vpair", s), {})
            last = None
            for r in range(2):
                base = trow(tix, r)
                P.last_w[("Kpair", s)], P.readers[("Kpair", s)] = prev_k, dict(rk)
                dma("act", Kpair[s][:, r * T:(r + 1) * T], recv_k[l][base:base + 128, :],
                    s_kv[s], writes=[("Kpair", s)], extra=[ck])
                P.last_w[("Vpair", s)], P.readers[("Vpair", s)] = prev_v, dict(rv)
                last = dma("act", Vpair[s][:, r, :, :],
                           recv_v[l][base:base + 128, :].rearrange("p (a f) -> p a f", a=8),
                           s_kv[s], writes=[("Vpair", s)], extra=[cv])
            P.last_w[("Kpair", s)] = last
            P.last_w[("Vpair", s)] = last

        def qz_proj(hp, s):
            wq = wneed()
            for half in range(2):
                r, rk = hrhs(half)
                b = yield from proj_gen(wq, r, rk)
                cs = slice(half * 512, (half + 1) * 512)
                C("dve", lambda e, b=b, s=s, cs=cs: e.tensor_single_scalar(
                    out=QA[s][0:64, cs], in_=pj[b][0:64, :], scalar=0.125, op=ALU.mult),
                  reads=[("pj", b)], writes=[("QA", s, half)])
                C("dve", lambda e, b=b, s=s, cs=cs: e.tensor_single_scalar(
                    out=QB[s][64:128, cs], in_=pj[b][64:128, :], scalar=0.125, op=ALU.mult),
                  reads=[("pj", b)], writes=[("QB", s, half)])
                yield None
            wz = wneed()
            for half in range(2):
                r, rk = hrhs(half)
                b = yield from proj_gen(wz, r, rk)
                cs = slice(half * 512, (half + 1) * 512)
                C("act", lambda e, b=b: e.activation(out=zs[:, :], in_=pj[b][:, :], func=AF.Exp, scale=-1.0),
                  reads=[("pj", b)], writes=["zs"])
                C("act", lambda e: e.activation(out=zs[:, :], in_=zs[:, :], func=AF.Ln, bias=1.0),
                  reads=["zs"], writes=["zs"])
                C("act", lambda e: e.activation(out=zs[:, :], in_=zs[:, :], func=AF.Exp, scale=-1.0),
                  reads=["zs"], writes=["zs"])
                C("dve", lambda e, b=b, s=s, cs=cs: e.tensor_tensor(out=SZ[s][:, cs], in0=pj[b][:, :],
                                                                 in1=zs[:, :], op=ALU.mult),
                  reads=[("pj", b), "zs"], writes=[("SZ", s, half)])
                yield None

        def finalize(hp, s, j, normalize, sink_col=None):
            cs = slice(j * 512, (j + 1) * 512)
            if normalize:
                if sink_col is not None:
                    C("act", lambda e: e.activation(out=rl[:, :], in_=pl[:, :], func=AF.Ln,
                                                    bias=esink[:, sink_col:sink_col + 1]),
                      reads=["pl", "esink"], writes=["rl"])
                else:
                    C("act", lambda e: e.activation(out=rl[:, :], in_=pl[:, :], func=AF.Ln), reads=["pl"], writes=["rl"])
                C("act", lambda e: e.activation(out=rl[:, :], in_=rl[:, :], func=AF.Exp, scale=-1.0),
                  reads=["rl"], writes=["rl"])
            for X in range(2):
                rs = slice(X * 64, (X + 1) * 64)
                if normalize:
                    C("dve", lambda e, X=X, rs=rs: e.tensor_tensor(out=f32a[X][rs, :], in0=pso[X][rs, :], in1=rl[rs, :],
                                                                 op=ALU.mult),
                      reads=[("pso", X), "rl"], writes=[("f32a", X)])
                    C("dve", lambda e, X=X, rs=rs, cs=cs: e.tensor_tensor(out=GT[rs, hp, cs], in0=f32a[X][rs, :],
                                                                        in1=SZ[s][rs, cs], op=ALU.mult),
                      reads=[("f32a", X), ("SZ", s, j)], writes=[("GT", hp, j)])
                else:
                    C("dve", lambda e, X=X, rs=rs, cs=cs: e.tensor_tensor(out=GT[rs, hp, cs], in0=pso[X][rs, :],
                                                                        in1=SZ[s][rs, cs], op=ALU.mult),
                      reads=[("pso", X), ("SZ", s, j)], writes=[("GT", hp, j)])

        def run3(tiles, stageA, stageB, bg, bgn):
            gens = {}
            N = len(tiles)

            def A1(k):
                gens[k] = stageA(tiles[k])
                next(gens[k])

            def A2(k):
                for _ in gens.pop(k):
                    pass

            A1(0)
            if N > 1:
                A1(1)
            A2(0)
            for n in range(N):
                if n + 2 < N:
                    A1(n + 2)
                if n + 1 < N:
                    A2(n + 1)
                stageB(tiles[n])
                if bg is not None:
                    for _ in range(bgn):
                        next(bg, None)

        def kcols(kb):
            return (kb % 2) * T + (kb // 2) * 128

        def attention_sb(hp, s, bg=None, bgn=1):
            tiles = []
            for j in range(2):
                kbs = list(range(8 * j + 7, -1, -1))
                for n, kb in enumerate(kbs):
                    for X in range(2):
                        m = kb - 8 * j
                        c0 = 0 if m < 0 else max(0, (m - 1 + 1) // 2)
                        tiles.append(dict(j=j, X=X, kb=kb, m=m, c0=c0, first=(n == 0), last=(n == len(kbs) - 1)))

            def stageA(t):
                j, X, kb, m, c0 = t["j"], t["X"], t["kb"], t["m"], t["c0"]
                zb = zc["i"] % 3
                zc["i"] += 1
                par2 = zc["i"] % 2
                t["zb"] = zb
                a = X
                t["a"] = a
                cl = slice(c0 * 128, 512)
                qs = slice(j * 512 + c0 * 128, (j + 1) * 512)
                Q = QA[s] if X == 0 else QB[s]
                qk = ("QA" if X == 0 else "QB", s, j)
                kc0 = kcols(kb)
                diag = m >= 0
                C("pe", lambda e: e.matmul(pz[zb][:, cl], lhsT=Kpair[s][:, kc0:kc0 + 128], rhs=Q[:, qs], start=True,
                                           stop=not diag),
                  reads=[("Kpair", s), qk], writes=[("pz", zb)])
                if diag:
                    mc = slice(c0 * 128, (c0 + 1) * 128)
                    C("pe", lambda e: e.matmul(pz[zb][:, mc], lhsT=ident_bf, rhs=maskb[:, m % 2, :], start=False, stop=True),
                      reads=["maskb"], writes=[("pz", zb)])
                yield
                if t["first"]:
                    C("dve", lambda e, a=a: e.memset(Sacc[a][:, :], 0.0), writes=[("Sacc", a)])
                    C("dve", lambda e, a=a: e.memset(Saccb[a][:, :], 0.0), writes=[("Saccb", a)])
                eb = par2
                C("act", lambda e: e.activation(out=f32a[eb][:, cl], in_=pz[zb][:, cl], func=AF.Exp),
                  reads=[("pz", zb)], writes=[("f32a", eb)])
                sb_ = par2
                t["sb"] = sb_
                C("act", lambda e: e.activation(out=bfa[sb_][:, cl], in_=f32a[eb][:, cl], func=AF.Ln, bias=1.0),
                  reads=[("f32a", eb)], writes=[("bfa", sb_)])
                C("pe", lambda e: e.matmul(pz[zb][:, cl], lhsT=negtri_bf, rhs=bfa[sb_][:, cl], start=False,
                                           stop=t["first"], skip_group_check=True),
                  reads=[("bfa", sb_)], writes=[("pz", zb)])
                if not t["first"]:
                    C("pe", lambda e: e.matmul(pz[zb][:, cl], lhsT=negones_bf, rhs=Saccb[a][:, cl], start=False, stop=True,
                                               skip_group_check=True),
                      reads=[("Saccb", a)], writes=[("pz", zb)])
                if not t["last"]:
                    C("dve", lambda e: e.tensor_tensor(out=Sacc[a][:, cl], in0=Sacc[a][:, cl], in1=bfa[sb_][:, cl],
                                                       op=ALU.add),
                      reads=[("Sacc", a), ("bfa", sb_)], writes=[("Sacc", a)])
                    C("dve", lambda e: e.tensor_copy(out=Saccb[a][:, cl], in_=Sacc[a][:, cl]),
                      reads=[("Sacc", a)], writes=[("Saccb", a)])

            def stageB(t):
                j, X, kb, c0, zb = t["j"], t["X"], t["kb"], t["c0"], t["zb"]
                cl = slice(c0 * 128, 512)
                ab = t["sb"]
                C("act", lambda e: e.activation(out=bfp[ab][:, cl], in_=pz[zb][:, cl], func=AF.Exp),
                  reads=[("pz", zb)], writes=[("bfp", ab)])
                if t["first"]:
                    C("pe", lambda e: e.matmul(pso[X][:, :], lhsT=zeros_bf, rhs=ident4_flat, start=True, stop=False),
                      writes=[("pso", X)])
                C("pe", lambda e: e.matmul(pso[X][:, cl], lhsT=Vpair[s][:, kb % 2, kb // 2, :], rhs=bfp[ab][:, cl], start=False,
                                           stop=t["last"]),
                  reads=[("Vpair", s), ("bfp", ab)], writes=[("pso", X)])
                if t["last"] and X == 1:
                    finalize(hp, s, j, normalize=False)

            return tiles, stageA, stageB

        def attention_fox(hp, s, bg=None, bgn=1):
            tiles = []
            for j in range(2):
                kbs = list(range(8 * j + 7, -1, -1))
                for n, kb in enumerate(kbs):
                    for X in range(2):
                        m = kb - 8 * j
                        c0 = 0 if m < 0 else max(0, m // 2)
                        tiles.append(dict(j=j, X=X, kb=kb, m=m, c0=c0, first=(n == 0), last=(n == len(kbs) - 1)))

            def stageA(t):
                j, X, kb, m, c0 = t["j"], t["X"], t["kb"], t["m"], t["c0"]
                h = 2 * hp + X
                zb = zc["i"] % 3
                zc["i"] += 1
                par2 = zc["i"] % 2
                t["zb"] = zb
                cl = slice(c0 * 128, 512)
                qs = slice(j * 512 + c0 * 128, (j + 1) * 512)
                Q = QA[s] if X == 0 else QB[s]
                qk = ("QA" if X == 0 else "QB", s, j)
                kc0 = kcols(kb)
                diag = m >= 0
                C("pe", lambda e: e.matmul(pz[zb][:, cl], lhsT=Kpair[s][:, kc0:kc0 + 128], rhs=Q[:, qs], start=True,
                                           stop=not diag),
                  reads=[("Kpair", s), qk], writes=[("pz", zb)])
                if diag:
                    mc = slice(c0 * 128, (c0 + 1) * 128)
                    C("pe", lambda e: e.matmul(pz[zb][:, mc], lhsT=ident_bf, rhs=maskb[:, m % 2, :], start=False, stop=True),
                      reads=["maskb"], writes=[("pz", zb)])
                yield
                if t["first"]:
                    b = pjc["i"] % 2
                    pjc["i"] += 1
                    Dt = [bfa[0], bfa[1], Saccb[0]]
                    Dk = [("bfa", 0), ("bfa", 1), ("Saccb", 0)]
                    for k in range(3):
                        cv = cqs[:, k, :].rearrange("p (a h) -> p a h", a=8)[:, 4 * j:4 * j + 4, h:h + 1]
                        C("dve", lambda e, k=k, cv=cv: e.tensor_tensor(
                            out=Dt[k][:, :].rearrange("p (c t) -> p c t", c=4), in0=ident4_bf,
                            in1=cv.to_broadcast([128, 4, 128]), op=ALU.mult),
                          reads=[("cqs", k)], writes=[Dk[k]])
                    for k in range(3):
                        C("pe", lambda e, b=b, k=k: e.matmul(pj[b][:, :], lhsT=ones_bf, rhs=Dt[k][:, :], start=(k == 0),
                                                           stop=(k == 2)),
                          reads=[Dk[k]], writes=[("pj", b)])
                    C("dve", lambda e, b=b: e.tensor_copy(out=cqb[X][:, :], in_=pj[b][:, :]),
                      reads=[("pj", b)], writes=[("Sacc", X)])
                fa = par2
                t["fa"] = fa
                C("dve", lambda e: e.tensor_tensor(out=f32a[fa][:, cl], in0=pz[zb][:, cl], in1=cqb[X][:, cl], op=ALU.add),
                  reads=[("pz", zb), ("Sacc", X)], writes=[("f32a", fa)])
                C("act", lambda e: e.activation(out=bfp[fa][:, cl], in_=f32a[fa][:, cl], func=AF.Exp,
                                                bias=ncum[:, kb, h:h + 1]),
                  reads=[("f32a", fa), "ncum"], writes=[("bfp", fa)])

            def stageB(t):
                j, X, kb, c0, fa = t["j"], t["X"], t["kb"], t["c0"], t["fa"]
                cl = slice(c0 * 128, 512)
                if t["first"]:
                    C("pe", lambda e: e.matmul(pso[X][:, :], lhsT=zeros_bf, rhs=ident4_flat, start=True, stop=False),
                      writes=[("pso", X)])
                    if X == 0:
                        C("pe", lambda e: e.matmul(pl[:, :], lhsT=zeros_bf, rhs=ident4_flat, start=True, stop=False),
                          writes=["pl"])
                C("pe", lambda e: e.matmul(pso[X][:, cl], lhsT=Vpair[s][:, kb % 2, kb // 2, :], rhs=bfp[fa][:, cl], start=False,
                                           stop=t["last"]),
                  reads=[("Vpair", s), ("bfp", fa)], writes=[("pso", X)])
                C("pe", lambda e: e.matmul(pl[:, cl], lhsT=(oneslo_bf if X == 0 else oneshi_bf), rhs=bfp[fa][:, cl],
                                           start=False, stop=(t["last"] and X == 1)),
                  reads=[("bfp", fa)], writes=["pl"])
                if t["last"] and X == 1:
                    finalize(hp, s, j, normalize=True)

            return tiles, stageA, stageB

        def attention_swa(hp, s, bg=None, bgn=1):
            tiles = []
            for j in range(2):
                for c in range(4):
                    for X in range(2):
                        i = 4 * j + c
                        ms = [m for m in range(3) if not (i == 0 and m == 0)]
                        tiles.append(dict(j=j, X=X, c=c, i=i, ms=ms, first=(c == 0), last=(c == 3)))

            def stageA(t):
                j, X, c, i, ms = t["j"], t["X"], t["c"], t["i"], t["ms"]
                h = 2 * hp + X
                zb = zc["i"] % 3
                zc["i"] += 1
                par2 = zc["i"] % 2
                t["zb"] = zb
                Q = QA[s] if X == 0 else QB[s]
                qk = ("QA" if X == 0 else "QB", s, j)
                qs = slice(i * 128, (i + 1) * 128)
                for n, m in enumerate(ms):
                    kb = 2 * i - 1 + m
                    kc0 = kcols(kb)
                    C("pe", lambda e, m=m, kc0=kc0, n=n: e.matmul(
                        pz[zb][:, m * 128:(m + 1) * 128], lhsT=Kpair[s][:, kc0:kc0 + 128], rhs=Q[:, qs],
                        start=(n == 0), stop=(n == len(ms) - 1)),
                      reads=[("Kpair", s), qk], writes=[("pz", zb)])
                yield
                ml = slice(ms[0] * 128, 384)
                fa = par2
                t["fa"] = fa
                C("dve", lambda e: e.scalar_tensor_tensor(out=f32a[fa][:, ml], in0=distA[:, ml], scalar=-slopes[h],
                                                          in1=pz[zb][:, ml], op0=ALU.mult, op1=ALU.add),
                  reads=[("pz", zb)], writes=[("f32a", fa)])
                C("act", lambda e: e.activation(out=bfp[fa][:, ml], in_=f32a[fa][:, ml], func=AF.Exp),
                  reads=[("f32a", fa)], writes=[("bfp", fa)])

            def stageB(t):
                j, X, c, i, ms, fa = t["j"], t["X"], t["c"], t["i"], t["ms"], t["fa"]
                oc = slice(c * 128, (c + 1) * 128)
                for n, m in enumerate(ms):
                    kb = 2 * i - 1 + m
                    C("pe", lambda e, m=m, kb=kb, n=n: e.matmul(
                        pso[X][:, oc], lhsT=Vpair[s][:, kb % 2, kb // 2, :], rhs=bfp[fa][:, m * 128:(m + 1) * 128],
                        start=(t["first"] and n == 0), stop=(t["last"] and n == len(ms) - 1)),
                      reads=[("Vpair", s), ("bfp", fa)], writes=[("pso", X)])
                for n, m in enumerate(ms):
                    C("pe", lambda e, m=m, n=n: e.matmul(
                        pl[:, oc], lhsT=(oneslo_bf if X == 0 else oneshi_bf), rhs=bfp[fa][:, m * 128:(m + 1) * 128],
                        start=(t["first"] and X == 0 and n == 0), stop=(t["last"] and X == 1 and n == len(ms) - 1)),
                      reads=[("bfp", fa)], writes=["pl"])
                if t["last"] and X == 1:
                    finalize(hp, s, j, normalize=True, sink_col=na_seen * 16 + hp)

            return tiles, stageA, stageB

        afn = {0: attention_swa, 1: attention_sb, 2: attention_fox}[kind]
        bgn = 3 if kind == 0 else 1
        items = []
        for hp in range(16):
            tl, sA, sB = afn(hp, hp % NBP)
            items += [(hp, t, sA, sB) for t in tl]
        gens, bgens, started = {}, {}, set()

        def start_pair(hp):
            if hp < 16 and hp not in started:
                started.add(hp)
                load_pair(hp, hp % NBP)
                bgens[hp] = qz_proj(hp, hp % NBP)

        def ready_pair(hp):
            start_pair(hp)
            g = bgens.pop(hp, None)
            if g is not None:
                for _ in g:
                    pass

        def A1(k):
            hp, t, sA, sB = items[k]
            ready_pair(hp)
            gens[k] = sA(t)
            next(gens[k])

        def A2(k):
            for _ in gens.pop(k):
                pass

        NI = len(items)
        A1(0)
        A1(1)
        A2(0)
        for n in range(NI):
            if n + 2 < NI:
                A1(n + 2)
            if n + 1 < NI:
                A2(n + 1)
            hp, t, sA, sB = items[n]
            sB(t)
            start_pair(hp + 1)
            g = bgens.get(hp + 1)
            if g is not None:
                for _ in range(bgn):
                    next(g, None)

        for half in range(2):
            cs = slice(half * 512, (half + 1) * 512)
            for m in range(KC):
                w = wneed()
                b = proj(w, (lambda kc, cs=cs: GT[:, kc, cs]), (lambda kc, half=half: [("GT", kc, half)]))
                C("act", lambda e, b=b, m=m: e.activation(out=ytmp[:, m, :], in_=pj[b][:, :], func=AF.Copy),
                  reads=[("pj", b)], writes=[("hT", m, 0), ("hT", m, 1)])
                sq = sqb[m % 2]
                C("act", lambda e, b=b, sq=sq: e.activation(out=sq[:, :], in_=pj[b][:, :], func=AF.Square),
                  reads=[("pj", b)], writes=[("bfa", m % 2)])
                C("pe", lambda e, m=m, sq=sq: e.matmul(pl[:, :], lhsT=ones_bf, rhs=sq[:, :], start=(m == 0),
                                                     stop=(m == KC - 1)),
                  reads=[("bfa", m % 2)], writes=["pl"])
            C("act", lambda e: e.activation(out=rstd[:, :], in_=pl[:, :], func=AF.Ln, scale=1.0 / D, bias=EPS),
              reads=["pl"], writes=["rl"])
            C("act", lambda e: e.activation(out=rstd[:, :], in_=rstd[:, :], func=AF.Exp, scale=-0.5),
              reads=["rl"], writes=["rl"])
            for m in range(KC):
                fa = m % 2
                C("dve", lambda e, m=m, fa=fa, gcol=gcol: e.scalar_tensor_tensor(
                    out=f32a[fa][:, :], in0=ytmp[:, m, :], scalar=gpost[:, gcol + m:gcol + m + 1], in1=rstd[:, :],
                    op0=ALU.mult, op1=ALU.mult),
                  reads=[("hT", m, 0), ("hT", m, 1), "rl"], writes=[("f32a", fa)])
                C("dve", lambda e, m=m, fa=fa, cs=cs: e.tensor_tensor(out=xT[:, m, cs], in0=xT[:, m, cs],
                                                                     in1=f32a[fa][:, :], op=ALU.add),
                  reads=[("xT", m, half), ("f32a", fa)], writes=[("xT", m, half)])
        if kind == 0:
            na_seen += 1

    for q in range(4):
        dma("sp", out_d[:, 4 * q * T:(4 * q + 4) * T].rearrange("p (a t) -> p a t", a=4), xT[:, 4 * q:4 * q + 4, :],
            s_out, reads=[("xT", kc, hf) for kc in range(4 * q, 4 * q + 4) for hf in range(2)])
    P.emit({"sp": [s_out]})
    es.close()
    return nc


def _tok_index(par):
    tl = np.arange(T)
    return (2 * (tl // 128) + par) * 128 + (tl % 128)


def _consts(par):
    bf = ml_dtypes.bfloat16
    j = np.arange(128)[:, None]
    s = np.arange(128)[None, :]
    ident = (j == s).astype(np.float32)
    ones = np.ones((128, 128), np.float32)
    negtri = -(j >= s).astype(np.float32)
    oneslo = np.zeros((128, 128), np.float32)
    oneslo[:, :64] = 1.0
    oneshi = np.zeros((128, 128), np.float32)
    oneshi[:, 64:] = 1.0
    triup = (j <= s).astype(np.float32)
    cbf = np.stack([ident, ones, negtri, -ones, oneslo, oneshi, triup, ident, ident, ident, ident, 0.0 * ident],
                   axis=1).reshape(128, 12 * 128).astype(bf)

    def zone_mask(strict):
        sl = np.arange(128)[:, None]
        tl = np.arange(128)[None, :]
        tri = np.where((sl < tl) if strict else (sl <= tl), 0.0, NEGBIG).astype(np.float32)
        zero = np.zeros((128, 128), np.float32)
        full = np.full((128, 128), NEGBIG, np.float32)
        ev, od = (tri, full) if par == 0 else (zero, tri)
        return np.stack([ev, od], axis=1).reshape(128, 2 * 128).astype(bf)

    maskSB = zone_mask(True)
    maskFX = zone_mask(False)
    dist = np.zeros((128, 3, 128), np.float32)
    sl = np.arange(128)[:, None]
    tl = np.arange(128)[None, :]
    for m in range(3):
        dd = (par + 1 - m) * 128 + tl - sl
        ok = (dd >= 0) & (dd < 128)
        dist[:, m, :] = np.where(ok, dd, 1.0e7)
    distA = dist.reshape(128, 384).astype(np.float32)
    parv = np.zeros((128, 2), np.float32)
    parv[:, 0] = 1.0 - par
    parv[:, 1] = float(par)
    return dict(cbf=cbf, maskSB=maskSB, maskFX=maskFX, distA=distA, par=parv)


def _weights_dev(l, w_in, w_out):
    plan = layer_tile_plan(l)
    out = np.zeros((len(plan), 2, 128, KC * 64), np.float32)
    for t, (kind, idx) in enumerate(plan):
        cols = tile_cols(l, kind, idx)
        n = len(cols)
        src = w_out if kind == "o" else w_in
        blk = np.zeros((128, KC, 128), np.float32)
        blk[:, :, :n] = src[:, cols].reshape(KC, 128, n).transpose(1, 0, 2)
        for hf in range(2):
            out[t, hf] = blk[:, :, hf * 64:(hf + 1) * 64].reshape(128, KC * 64)
    return out


def _vec_layout(v):
    return np.ascontiguousarray(v.reshape(DEPTH, KC, 128).transpose(2, 0, 1).reshape(128, DEPTH * KC)).astype(np.float32)


def _run(layers, x, g_pre, g_post, w_in_a, w_out_a, sinks_a, w_in_b, w_out_b, w_in_c, b_f_c, w_out_c):
    x = np.asarray(x, np.float32)
    nc = build_program(layers)
    wins = {0: (w_in_a[0], w_out_a[0]), 1: (w_in_b[0], w_out_b[0]), 2: (w_in_c[0], w_out_c[0]), 3: (w_in_a[1], w_out_a[1])}
    wdev = {l: _weights_dev(l, np.asarray(wins[l][0], np.float32), np.asarray(wins[l][1], np.float32)) for l in layers}
    gpre = _vec_layout(np.asarray(g_pre, np.float32))
    gpost = _vec_layout(np.asarray(g_post, np.float32))
    sk = np.zeros((128, 32), np.float32)
    sa = np.asarray(sinks_a, np.float32)
    for n in range(2):
        for hp in range(16):
            sk[:64, n * 16 + hp] = sa[n, 2 * hp]
            sk[64:, n * 16 + hp] = sa[n, 2 * hp + 1]
    bfb = np.tile(np.asarray(b_f_c, np.float32)[0][None, None, :], (128, 8, 1)).reshape(128, 256)
    consts = [_consts(0), _consts(1)]
    in_maps = []
    for c in range(8):
        b, par = c // 2, c % 2
        tok = _tok_index(par)
        xs = x[b][tok]
        xT = np.ascontiguousarray(xs.reshape(T, KC, 128).transpose(2, 1, 0)).reshape(128, KC * T)
        m = dict(xT=xT, gpre=gpre, gpost=gpost, sinks=sk, bfb=bfb)
        m.update(consts[par])
        for l in layers:
            m[f"w{l}"] = wdev[l]
        in_maps.append(m)
    res = run_bass_kernel_spmd(nc, in_maps, core_ids=list(range(8)))
    out = np.empty_like(x)
    for c in range(8):
        b, par = c // 2, c % 2
        tok = _tok_index(par)
        oT = np.asarray(res.results[c]["outT"], np.float32).reshape(128, KC, T)
        out[b][tok] = oT.transpose(2, 1, 0).reshape(T, D)
    return out


def kernel(x, g_pre, g_post, w_in_a, w_out_a, sinks_a, w_in_b, w_out_b, w_in_c, b_f_c, w_out_c):
    return _run([0, 1, 2, 3], x, g_pre, g_post, w_in_a, w_out_a, sinks_a, w_in_b, w_out_b, w_in_c, b_f_c, w_out_c)
```
